# Optimizing a Trainium2 kernel written in Bass

```python
import jax, jax.numpy as jnp
from jax import lax
import numpy as np

D_MODEL = 2048
BATCH = 4
SEQ = 2048
DEPTH = 4

CHUNK = 64
GDN_HEAD_DIM = 128
GDN_HEADS = D_MODEL // 256
GDN_WIDTH = GDN_HEADS * GDN_HEAD_DIM
CONV_WIDTH = 4
GMLP_WIDTH = D_MODEL // 2
GMLP_GROUPS = 8
GMLP_GROUP_DIM = GMLP_WIDTH // GMLP_GROUPS
GMLP_BLOCK = 128
SBA_HEAD_DIM = 128
SBA_HEADS = D_MODEL // 256
SBA_WIDTH = SBA_HEADS * SBA_HEAD_DIM
QUERY_BLOCK = 128
N_BRANCHES = 3
D_FF = 4 * D_MODEL
EPS = 1e-6
PROJ_SIZES = (3 * GDN_WIDTH,
              GDN_HEADS,
              GDN_HEADS,
              GDN_WIDTH,
              2 * GMLP_WIDTH,
              3 * SBA_WIDTH,
              N_BRANCHES * D_MODEL)
D_IN = sum(PROJ_SIZES)

kernel_name = "hybrid_gdn_gmlp_stickbreak_trunk"


def rms_norm(x, gain):
    xf = x.astype(jnp.float32)
    y = xf * lax.rsqrt(jnp.mean(xf * xf, axis=-1, keepdims=True) + EPS)
    return (y * gain.astype(jnp.float32)).astype(x.dtype)


def layer_norm(x, gain):
    xf = x.astype(jnp.float32)
    mu = jnp.mean(xf, axis=-1, keepdims=True)
    xc = xf - mu
    y = xc * lax.rsqrt(jnp.mean(xc * xc, axis=-1, keepdims=True) + EPS)
    return (y * gain.astype(jnp.float32)).astype(x.dtype)


def l2_norm(x):
    return x * lax.rsqrt(jnp.sum(x * x, axis=-1, keepdims=True) + EPS)


def causal_depthwise_conv(x, w):
    K, C = w.shape
    return lax.conv_general_dilated(x, w[:, None, :].astype(x.dtype), window_strides=(1,),
                                    padding=[(K - 1, 0)],
                                    dimension_numbers=('NWC', 'WIO', 'NWC'),
                                    feature_group_count=C)


def gated_delta_rule_chunked(q, k, v, g, beta):
    out_dtype = v.dtype
    B, H, T, dk = q.shape
    dv = v.shape[-1]
    N = T // CHUNK
    f32 = jnp.float32
    q = q.astype(f32).reshape(B, H, N, CHUNK, dk)
    k = k.astype(f32).reshape(B, H, N, CHUNK, dk)
    v = v.astype(f32).reshape(B, H, N, CHUNK, dv)
    g = jnp.cumsum(g.astype(f32).reshape(B, H, N, CHUNK), axis=-1)
    beta = beta.astype(f32).reshape(B, H, N, CHUNK)
    incl = jnp.tril(jnp.ones((CHUNK, CHUNK), dtype=bool))
    strict = jnp.tril(jnp.ones((CHUNK, CHUNK), dtype=bool), -1)
    decay = jnp.exp(jnp.where(incl, g[..., :, None] - g[..., None, :], -jnp.inf))
    kb = k * beta[..., None]
    L = jnp.where(strict, jnp.einsum('bhncd,bhnsd->bhncs', kb, k) * decay, 0.0)
    eye = jnp.eye(CHUNK, dtype=f32)
    rhs = jnp.concatenate([v * beta[..., None], kb * jnp.exp(g)[..., None]], axis=-1)
    sol = lax.linalg.triangular_solve(L + eye, rhs, left_side=True, lower=True,
                                      unit_diagonal=True)
    u, w = sol[..., :dv], sol[..., dv:]
    intra = jnp.einsum('bhncd,bhnsd->bhncs', q, k) * decay
    q_dec = q * jnp.exp(g)[..., None]
    g_last = g[..., -1]
    k_dec = k * jnp.exp(g_last[..., None] - g)[..., None]
    xs = tuple(jnp.moveaxis(t, 2, 0) for t in (u, w, intra, q_dec, k_dec, g_last))

    def step(S, inp):
        u_n, w_n, a_n, qd_n, kd_n, gl_n = inp
        v_new = u_n - jnp.einsum('bhck,bhkv->bhcv', w_n, S)
        o_n = jnp.einsum('bhck,bhkv->bhcv', qd_n, S) + jnp.einsum('bhcs,bhsv->bhcv', a_n, v_new)
        S = S * jnp.exp(gl_n)[..., None, None] + jnp.einsum('bhck,bhcv->bhkv', kd_n, v_new)
        return S, o_n

    S0 = jnp.zeros((B, H, dk, dv), f32)
    _, o = lax.scan(step, S0, xs)
    return jnp.moveaxis(o, 0, 2).reshape(B, H, T, dv).astype(out_dtype)


def stick_breaking_attention(q, k, v):
    B, H, T, d = q.shape
    nb = T // QUERY_BLOCK
    qb = jnp.moveaxis(q.reshape(B, H, nb, QUERY_BLOCK, d), 2, 0)
    kpos = jnp.arange(T)
    scale = d ** -0.5

    def block(args):
        q_blk, i = args
        z = jnp.einsum('bhqd,bhkd->bhqk', q_blk, k).astype(jnp.float32) * scale
        qpos = i * QUERY_BLOCK + jnp.arange(QUERY_BLOCK)
        strict = kpos[None, :] < qpos[:, None]
        log_keep = jnp.where(strict, jax.nn.log_sigmoid(-z), 0.0)
        suffix = lax.cumsum(log_keep, axis=3, reverse=True) - log_keep
        A = jnp.where(strict, jnp.exp(jax.nn.log_sigmoid(z) + suffix), 0.0)
        return jnp.einsum('bhqk,bhkd->bhqd', A.astype(v.dtype), v)

    out = lax.map(block, (qb, jnp.arange(nb)))
    return jnp.moveaxis(out, 0, 2).reshape(B, H, T, d)


def hybrid_mixer(h, w_in, conv_w, a_log, dt_bias, gdn_norm_g, gmlp_ln_g, w_spatial, b_spatial,
                 sba_q_g, sba_k_g, w_out_a, w_out_b, w_out_c, w_out):
    B, T, _ = h.shape
    z = h @ w_in
    split_idx = np.cumsum(PROJ_SIZES)[:-1].tolist()
    gdn_qkv, gdn_a, gdn_b, gdn_gate, gmlp_uv, sba_qkv, gate_logits = jnp.split(z, split_idx, axis=-1)

    qkv = jax.nn.silu(causal_depthwise_conv(gdn_qkv, conv_w))
    qa, ka, va = jnp.split(qkv, 3, axis=-1)
    to_heads = lambda t, H, d: jnp.transpose(t.reshape(B, T, H, d), (0, 2, 1, 3))
    qa = l2_norm(to_heads(qa, GDN_HEADS, GDN_HEAD_DIM).astype(jnp.float32)) * GDN_HEAD_DIM ** -0.5
    ka = l2_norm(to_heads(ka, GDN_HEADS, GDN_HEAD_DIM).astype(jnp.float32))
    va = to_heads(va, GDN_HEADS, GDN_HEAD_DIM)
    beta = jnp.transpose(jax.nn.sigmoid(gdn_b.astype(jnp.float32)), (0, 2, 1))
    g = -jnp.exp(a_log.astype(jnp.float32)) * jax.nn.softplus(
        gdn_a.astype(jnp.float32) + dt_bias.astype(jnp.float32))
    g = jnp.transpose(g, (0, 2, 1))
    oa = gated_delta_rule_chunked(qa, ka, va, g, beta)
    oa = jnp.transpose(oa, (0, 2, 1, 3))
    oa = rms_norm(oa, gdn_norm_g) * jax.nn.silu(gdn_gate.reshape(B, T, GDN_HEADS, GDN_HEAD_DIM))
    branch_a = oa.reshape(B, T, GDN_WIDTH) @ w_out_a

    uv = jax.nn.gelu(gmlp_uv, approximate=False)
    u, vb = jnp.split(uv, 2, axis=-1)
    vb = layer_norm(vb, gmlp_ln_g).reshape(B, T // GMLP_BLOCK, GMLP_BLOCK, GMLP_GROUPS, GMLP_GROUP_DIM)
    pos = jnp.arange(GMLP_BLOCK) // CHUNK
    chunk_causal = pos[None, :] <= pos[:, None]
    ws = jnp.where(chunk_causal[None], w_spatial, 0.0).astype(vb.dtype)
    s = jnp.einsum('gts,bnsgc->bntgc', ws, vb) + jnp.transpose(b_spatial)[None, None, :, :, None]
    branch_b = (u * s.reshape(B, T, GMLP_WIDTH)) @ w_out_b

    qc, kc, vc = jnp.split(sba_qkv, 3, axis=-1)
    qc = to_heads(rms_norm(qc.reshape(B, T, SBA_HEADS, SBA_HEAD_DIM), sba_q_g).reshape(B, T, SBA_WIDTH), SBA_HEADS, SBA_HEAD_DIM)
    kc = to_heads(rms_norm(kc.reshape(B, T, SBA_HEADS, SBA_HEAD_DIM), sba_k_g).reshape(B, T, SBA_WIDTH), SBA_HEADS, SBA_HEAD_DIM)
    vc = to_heads(vc, SBA_HEADS, SBA_HEAD_DIM)
    oc = stick_breaking_attention(qc, kc, vc)
    branch_c = jnp.transpose(oc, (0, 2, 1, 3)).reshape(B, T, SBA_WIDTH) @ w_out_c

    gates = jax.nn.sigmoid(gate_logits).reshape(B, T, N_BRANCHES, D_MODEL)
    y = gates[:, :, 0] * branch_a + gates[:, :, 1] * branch_b + gates[:, :, 2] * branch_c
    return y @ w_out


def setup_inputs(seed: int = 0) -> dict:
    key = jax.random.key(seed)
    ks = jax.random.split(key, 20)
    nrm = lambda k, shape, scale: jax.random.normal(k, shape, jnp.float32) * scale
    gain = lambda k, shape: 1.0 + 0.02 * jax.random.normal(k, shape, jnp.float32)
    dt = jnp.exp(jax.random.uniform(ks[4], (DEPTH, GDN_HEADS), jnp.float32,
                                    np.log(1e-3).astype(np.float32), np.log(1e-1).astype(np.float32)))
    return {
        "x": nrm(ks[0], (BATCH, SEQ, D_MODEL), 1.0),
        "w_in": nrm(ks[1], (DEPTH, D_MODEL, D_IN), D_MODEL ** -0.5),
        "conv_w": nrm(ks[2], (DEPTH, CONV_WIDTH, 3 * GDN_WIDTH), CONV_WIDTH ** -0.5),
        "a_log": jnp.log(jax.random.uniform(ks[3], (DEPTH, GDN_HEADS), jnp.float32, 1.0, 16.0)),
        "dt_bias": dt + jnp.log(-jnp.expm1(-dt)),
        "gdn_norm_g": gain(ks[5], (DEPTH, GDN_HEAD_DIM)),
        "gmlp_ln_g": gain(ks[6], (DEPTH, GMLP_WIDTH)),
        "w_spatial": nrm(ks[7], (DEPTH, GMLP_GROUPS, GMLP_BLOCK, GMLP_BLOCK), GMLP_BLOCK ** -0.5),
        "b_spatial": gain(ks[8], (DEPTH, GMLP_GROUPS, GMLP_BLOCK)),
        "sba_q_g": gain(ks[9], (DEPTH, SBA_HEAD_DIM)),
        "sba_k_g": gain(ks[10], (DEPTH, SBA_HEAD_DIM)),
        "w_out_a": nrm(ks[11], (DEPTH, GDN_WIDTH, D_MODEL), GDN_WIDTH ** -0.5),
        "w_out_b": nrm(ks[12], (DEPTH, GMLP_WIDTH, D_MODEL), GMLP_WIDTH ** -0.5),
        "w_out_c": nrm(ks[13], (DEPTH, SBA_WIDTH, D_MODEL), SBA_WIDTH ** -0.5),
        "w_out": nrm(ks[14], (DEPTH, D_MODEL, D_MODEL), D_MODEL ** -0.5),
        "norm_mix_g": gain(ks[15], (DEPTH, D_MODEL)),
        "norm_mlp_g": gain(ks[16], (DEPTH, D_MODEL)),
        "w_ff1": nrm(ks[17], (DEPTH, D_MODEL, D_FF), D_MODEL ** -0.5),
        "w_ff2": nrm(ks[18], (DEPTH, D_FF, D_MODEL), D_FF ** -0.5),
    }


def reference(x, w_in, conv_w, a_log, dt_bias, gdn_norm_g, gmlp_ln_g, w_spatial, b_spatial,
              sba_q_g, sba_k_g, w_out_a, w_out_b, w_out_c, w_out, norm_mix_g, norm_mlp_g,
              w_ff1, w_ff2):
    for l in range(DEPTH):
        h = rms_norm(x, norm_mix_g[l])
        x = x + hybrid_mixer(h, w_in[l], conv_w[l], a_log[l], dt_bias[l], gdn_norm_g[l],
                             gmlp_ln_g[l], w_spatial[l], b_spatial[l], sba_q_g[l], sba_k_g[l],
                             w_out_a[l], w_out_b[l], w_out_c[l], w_out[l])
        h = rms_norm(x, norm_mlp_g[l])
        x = x + jnp.square(jax.nn.relu(h @ w_ff1[l])) @ w_ff2[l]
    return x
```

```python
import contextlib
import os
import numpy as np
import concourse.bass as bass
import concourse.mybir as mybir
from concourse.bass_utils import run_bass_kernel_spmd

F32 = mybir.dt.float32
BF16 = mybir.dt.bfloat16
AF = mybir.ActivationFunctionType
ALU = mybir.AluOpType
AX = mybir.AxisListType

T = 2048
TL = 1024
D = 2048
L = 4
DIN = 15376
DFF = 8192
NCORES = 8
EPS = 1e-6
NEG = -30000.0
NPP = 131


def _h(t):
    return t.tensor if hasattr(t, "tensor") else t


class Buf:
    __slots__ = ("name", "w", "r", "sem", "cnt")

    def __init__(self, name):
        self.name = name
        self.w = None
        self.r = {}
        self.sem = None
        self.cnt = 0


class Tl:
    def __init__(self, t, shape, buf):
        self.t = _h(t)
        self.shape = list(shape)
        self.F = int(np.prod(shape[1:]))
        self.b = buf

    def v(self, p0, pn, off, dims):
        return bass.AP(self.t, p0 * self.F + off, [[self.F, pn]] + [list(d) for d in dims])

    def full(self):
        return self.v(0, self.shape[0], 0, [[1, self.F]])


class Prog:
    ENGS = ("pe", "act", "dve", "pool", "sp")

    def __init__(self, nc, stack):
        self.nc = nc
        self.stack = stack
        self.sems = {}
        for e in self.ENGS:
            self.sems[e] = stack.enter_context(nc.semaphore("s_" + e))
        self.cnt = {e: 0 for e in self.ENGS}
        self.ops = {e: [] for e in self.ENGS}
        self.seen = {e: {} for e in self.ENGS}
        self.dirty = {}
        self.uid = 0
        self.sem_pool = []
        self.nsem = 0

    def buf(self, name="b"):
        self.uid += 1
        return Buf(f"{name}{self.uid}")

    def _filter(self, eng, deps, barrier=False):
        out = []
        seen = self.seen[eng]
        for k, v in deps.items():
            if (not barrier) and eng == "pe" and k == "pe":
                continue
            if seen.get(k, 0) >= v:
                continue
            seen[k] = v
            out.append((k, v))
        return out

    def op(self, eng, fn, reads=(), writes=()):
        deps = {}

        def add(d):
            if d is not None and deps.get(d[0], 0) < d[1]:
                deps[d[0]] = d[1]

        for b in reads:
            add(b.w)
        for b in writes:
            add(b.w)
            for kv in b.r.items():
                add(kv)
        waits = self._filter(eng, deps)
        self.cnt[eng] += 1
        c = self.cnt[eng]
        for b in reads:
            b.r[eng] = c
        for b in writes:
            b.w = (eng, c)
            b.r = {}
        self.ops[eng].append((waits, fn, eng, 1))

    def pe(self, fn, r=(), w=()):
        self.op("pe", fn, r, w)

    def act(self, fn, r=(), w=()):
        self.op("act", fn, r, w)

    def dve(self, fn, r=(), w=()):
        self.op("dve", fn, r, w)

    def pool(self, fn, r=(), w=()):
        self.op("pool", fn, r, w)

    def _getsem(self, b):
        if b.sem is None:
            if self.sem_pool:
                b.sem, b.cnt = self.sem_pool.pop()
            else:
                self.nsem += 1
                b.sem = f"d{self.nsem}"
                self.sems[b.sem] = self.stack.enter_context(self.nc.semaphore(b.sem))
                b.cnt = 0

    def dma(self, q, out_ap, in_ap, dst, src, **kw):
        self._getsem(dst)
        deps = {}

        def add(d):
            if d is not None and deps.get(d[0], 0) < d[1]:
                deps[d[0]] = d[1]

        add(src.w)
        if dst.w is not None and not (dst.w[0] == dst.sem and not dst.r):
            add(dst.w)
        for kv in dst.r.items():
            add(kv)
        waits = self._filter(q, deps)
        dst.cnt += 16
        src.r[dst.sem] = dst.cnt
        dst.w = (dst.sem, dst.cnt)
        dst.r = {}
        self.dirty[dst.sem] = dst.cnt
        self.ops[q].append((waits, (lambda e: e.dma_start(out=out_ap, in_=in_ap, **kw)), dst.sem, 16))

    def cc(self, in_ap, out_ap, dst, src, groups):
        self._getsem(dst)
        deps = {}

        def add(d):
            if d is not None and deps.get(d[0], 0) < d[1]:
                deps[d[0]] = d[1]

        add(src.w)
        add(dst.w)
        for kv in dst.r.items():
            add(kv)
        waits = self._filter("pool", deps)
        dst.cnt += 1
        src.r[dst.sem] = dst.cnt
        dst.w = (dst.sem, dst.cnt)
        dst.r = {}
        self.dirty[dst.sem] = dst.cnt
        self.ops["pool"].append((waits, (lambda e: e.collective_compute(
            "AllGather", ALU.bypass, replica_groups=groups, ins=[in_ap], outs=[out_ap])), dst.sem, 1))

    def barrier(self):
        targets = {e: self.cnt[e] for e in self.ENGS if self.cnt[e] > 0}
        targets.update(self.dirty)
        self.dirty = {}
        for e in self.ENGS:
            waits = self._filter(e, targets, barrier=True)
            if waits:
                self.ops[e].append((waits, None, None, 0))

    def emit(self):
        ops = self.ops
        self.ops = {e: [] for e in self.ENGS}
        sems = self.sems

        def mk(eng):
            def body(e):
                for waits, fn, sk, inc in ops[eng]:
                    for k, v in waits:
                        e.wait_ge(sems[k], v)
                    if fn is not None:
                        fn(e).then_inc(sems[sk], inc)
            return body

        with self.nc.Block() as blk:
            blk.tensor(mk("pe"))
            blk.scalar(mk("act"))
            blk.vector(mk("dve"))
            blk.gpsimd(mk("pool"))
            blk.sync(mk("sp"))


class Frame:
    def __init__(self, P):
        self.P = P
        self.st = contextlib.ExitStack()
        self.bufs = []

    def __enter__(self):
        return self

    def tile(self, name, shape, dt):
        P = self.P
        P.uid += 1
        t = self.st.enter_context(P.nc.sbuf_tensor(f"{name}_{P.uid}", list(shape), dt))
        b = P.buf(name)
        self.bufs.append(b)
        return Tl(t, shape, b)

    def __exit__(self, et, ev, tb):
        if et is None:
            self.P.barrier()
            for b in self.bufs:
                if b.sem is not None:
                    self.P.sem_pool.append((b.sem, b.cnt))
        self.st.close()
        return False


def MM(out, lhsT, rhs, start=True, stop=True):
    return lambda e: e.matmul(out, lhsT=lhsT, rhs=rhs, start=start, stop=stop)


def TR(out, in_, ident):
    return lambda e: e.transpose(out, in_, ident)


def ACT(out, in_, func, bias=None, scale=None):
    kw = {}
    if bias is not None:
        kw["bias"] = bias
    if scale is not None:
        kw["scale"] = scale
    return lambda e: e.activation(out=out, in_=in_, func=func, **kw)


def TT(out, in0, in1, op):
    return lambda e: e.tensor_tensor(out=out, in0=in0, in1=in1, op=op)


def TS(out, in0, s1, s2, op0, op1=None):
    if op1 is None:
        return lambda e: e.tensor_scalar(out=out, in0=in0, scalar1=s1, scalar2=None, op0=op0)
    return lambda e: e.tensor_scalar(out=out, in0=in0, scalar1=s1, scalar2=s2, op0=op0, op1=op1)


def STT(out, in0, scalar, in1, op0, op1):
    return lambda e: e.scalar_tensor_tensor(out=out, in0=in0, scalar=scalar, in1=in1, op0=op0, op1=op1)


def CP(out, in_):
    return lambda e: e.tensor_copy(out=out, in_=in_)


C_ID, C_ONE, C_TRIU, C_MBL, C_MBU, C_NSTR, C_F32END = 0, 128, 256, 384, 448, 512, 576
CB_ONE, CB_TRIS, CB_MASK, CB_END = 0, 128, 256, 256 + 2048


def _const_packs():
    cf = np.zeros((128, C_F32END), np.float32)
    cf[:, C_ID:C_ID + 128] = np.eye(128, dtype=np.float32)
    cf[:, C_ONE:C_ONE + 128] = 1.0
    j = np.arange(128)[:, None]
    c = np.arange(128)[None, :]
    cf[:, C_TRIU:C_TRIU + 128] = (j <= c).astype(np.float32)
    p = np.arange(64)[:, None]
    f = np.arange(64)[None, :]
    cf[0:64, C_MBL:C_MBL + 64] = np.where(f <= p, 0.0, NEG)
    cf[0:64, C_MBU:C_MBU + 64] = np.where(f >= p, 0.0, NEG)
    cf[0:64, C_NSTR:C_NSTR + 64] = np.where(f < p, -1.0, 0.0)
    cb = np.zeros((128, CB_END), np.float32)
    cb[:, CB_ONE:CB_ONE + 128] = 1.0
    cb[:, CB_TRIS:CB_TRIS + 128] = (j >= c).astype(np.float32)
    pp = np.arange(128)[:, None]
    ff = np.arange(512)[None, :]
    for i in range(4):
        cb[:, CB_MASK + i * 512:CB_MASK + (i + 1) * 512] = (ff > i * 128 + pp).astype(np.float32)
    return cf, cb


def build(n_layers=L, stop_after=None, debug=False, n_cores=NCORES, QA=2, QD=1, XQ=64):
    nc = bass.Bass("TRN2", target_bir_lowering=False)
    skind = "ExternalOutput" if debug else "Internal"
    groups = [[2 * i, 2 * i + 1] for i in range(n_cores // 2)]
    pid = nc.partition_id()
    rank = pid % 2
    other = 1 - rank

    def din(name, shape, dt=F32):
        return nc.dram_tensor(name, list(shape), dt, kind="ExternalInput").ap()

    def dscr(name, shape, dt, internal=False):
        return nc.dram_tensor(name, list(shape), dt, kind=("Internal" if internal else skind)).ap()

    x_in = din("x", [TL, D])
    w_in = din("w_in", [L, D, DIN])
    w_oa = din("w_out_a", [L, 1024, D])
    w_ob = din("w_out_b", [L, 1024, D])
    w_oc = din("w_out_c", [L, 1024, D])
    w_o = din("w_out", [L, D, D])
    w_f1 = din("w_ff1", [L, D, DFF])
    w_f2 = din("w_ff2", [L, DFF, D])
    cf_in = din("cf", [128, C_F32END])
    cb_in = din("cb", [128, CB_END])
    pp_in = din("pp", [128, L * NPP])
    cw_in = din("cw", [L * 2 * 128, 48])
    ab_in = din("ab", [1, L * 16])
    lng_in = din("lng", [L, 1024])
    wsT_in = din("wsT", [L, 128, 1024])
    bsp_in = din("bsp", [L, 1024])
    y_out = nc.dram_tensor("y", [TL, D], F32, kind="ExternalOutput").ap()

    ZL = [dscr(f"ZL{k}", [1024, TL], F32) for k in range(4)]
    SENDZ = [dscr(f"SENDZ{k}", [512, TL], F32, True) for k in range(4)]
    GZ = [dscr(f"GZ{k}", [1024, TL], F32, True) for k in range(4)]
    FZ = [dscr(f"FZ{k}", [1024, TL], F32) for k in range(4)]
    SQL = [dscr(f"SQL{k}", [1024, TL], BF16) for k in range(2)]
    SENDQ = [dscr(f"SENDQ{k}", [512, TL], BF16, True) for k in range(2)]
    GQ = [dscr(f"GQ{k}", [1024, TL], BF16, True) for k in range(2)]
    FSQ = [dscr(f"FSQ{k}", [1024, TL], BF16) for k in range(2)]
    SVL = dscr("SVL", [2 * TL, 512], BF16)
    SENDV = dscr("SENDV", [TL, 512], BF16, True)
    GV = dscr("GV", [2 * TL, 512], BF16, True)
    FSV = dscr("FSV", [2 * TL, 512], BF16)
    GBL = dscr("GBL", [2 * TL, 8], F32)
    SENDG = dscr("SENDG", [TL, 8], F32, True)
    GGB = dscr("GGB", [2 * TL, 8], F32, True)
    FGB = dscr("FGB", [2 * TL, 8], F32)
    QT = dscr("QT", [4, 128, T], F32)
    KT = dscr("KT", [4, 128, T], F32)
    VT = dscr("VT", [4, 128, T], F32)
    OAm = dscr("OAm", [1024, TL], BF16)
    OCm = dscr("OCm", [1024, TL], BF16)
    SENDO = [dscr(f"SENDO{k}", [512, TL], BF16, True) for k in range(2)]
    GO = [dscr(f"GO{k}", [1024, TL], BF16, True) for k in range(2)]
    BRA = dscr("BRA", [1024, TL], BF16)
    BRC = dscr("BRC", [1024, TL], BF16)
    UT = dscr("UT", [8, 128, TL], F32)
    UBT = dscr("UBT", [8, 128, TL], BF16)
    GATES = dscr("GATES", [48, 128, TL], BF16)
    XS = [dscr(f"XS{i}", [TL, D], F32) for i in range(3)]

    with contextlib.ExitStack() as stack:
        P = Prog(nc, stack)
        PS = [_h(stack.enter_context(nc.psum_tensor(f"ps{i}", [128, 1024], F32))) for i in range(4)]
        PB = [P.buf(f"psb{i}") for i in range(8)]

        def psv(b, p0, pn, off, dims):
            return bass.AP(PS[b // 2], p0 * 1024 + (b % 2) * 512 + off, [[1024, pn]] + [list(d) for d in dims])

        DB = {}

        def db(name):
            if name not in DB:
                DB[name] = P.buf(name)
            return DB[name]

        with Frame(P) as G:
            CF = G.tile("CF", [128, C_F32END], F32)
            CBt = G.tile("CB", [128, CB_END], BF16)
            PPt = G.tile("PP", [128, L * NPP], F32)
            ABt = G.tile("AB", [128, L * 16], F32)
            NEA = G.tile("NEA", [128, L * 8], F32)
            SQS = G.tile("SQS", [128, L], F32)
            P.dma("sp", CF.full(), cf_in, CF.b, db("cst"))
            P.dma("pool", CBt.full(), cb_in, CBt.b, db("cst"))
            P.dma("sp", PPt.full(), pp_in, PPt.b, db("cst"))
            P.dma("sp", ABt.full(), bass.AP(ab_in.tensor, 0, [[0, 128], [1, L * 16]]), ABt.b, db("cst"))
            for l in range(L):
                P.act(ACT(NEA.v(0, 128, l * 8, [[1, 8]]), ABt.v(0, 128, l * 16, [[1, 8]]), AF.Exp), [ABt.b], [NEA.b])
            P.dve(TS(NEA.full(), NEA.full(), -1.0, None, ALU.mult), [NEA.b], [NEA.b])
            for l in range(L):
                P.dve(TS(SQS.v(0, 128, l, [[1, 1]]), PPt.v(0, 128, l * NPP + 129, [[1, 1]]), float(128 ** -0.5), None, ALU.mult),
                      [PPt.b], [SQS.b])

            ident = CF.v(0, 128, C_ID, [[1, 128]])
            ones_f = CF.v(0, 128, C_ONE, [[1, 128]])
            ones_b = CBt.v(0, 128, CB_ONE, [[1, 128]])
            EPSc = float(EPS)

            def norm_T(fr, xsrc, xbuf, tok0, ntiles, gcol, hT, tcol0):
                xt = [fr.tile("xt", [128, D], F32) for _ in range(2)]
                sq = fr.tile("sq", [128, D], F32)
                xs = [fr.tile("xs", [128, D], F32) for _ in range(2)]
                st = [fr.tile("st", [128, 4], F32) for _ in range(2)]
                NTK = hT.shape[2]
                for i in range(ntiles):
                    a = i % 2
                    X, S_, Xs = xt[a], st[a], xs[a]
                    r0 = tok0 + i * 128
                    P.dma("sp", X.full(), xsrc[r0:r0 + 128, :], X.b, xbuf)
                    P.act(ACT(sq.full(), X.full(), AF.Square), [X.b], [sq.b])
                    P.dve(lambda e, o=S_.v(0, 128, 0, [[1, 1]]), i_=sq.full(): e.reduce_sum(out=o, in_=i_, axis=AX.X), [sq.b], [S_.b])
                    P.act(ACT(S_.v(0, 128, 1, [[1, 1]]), S_.v(0, 128, 0, [[1, 1]]), AF.Ln, bias=EPSc, scale=1.0 / D), [S_.b], [S_.b])
                    P.act(ACT(S_.v(0, 128, 2, [[1, 1]]), S_.v(0, 128, 1, [[1, 1]]), AF.Exp, scale=-0.5), [S_.b], [S_.b])
                    P.act(ACT(Xs.full(), X.full(), AF.Copy, scale=S_.v(0, 128, 2, [[1, 1]])), [X.b, S_.b], [Xs.b])
                    bb = 4 * a
                    for kc in range(16):
                        b = bb + kc // 4
                        P.pe(TR(psv(b, 0, 128, (kc % 4) * 128, [[1, 128]]), Xs.v(0, 128, kc * 128, [[1, 128]]), ident),
                             [Xs.b, CF.b], [PB[b]])
                    for q in range(4):
                        b = bb + q
                        P.dve(TT(hT.v(0, 128, (4 * q) * NTK + tcol0 + i * 128, [[NTK, 4], [1, 128]]),
                                 psv(b, 0, 128, 0, [[128, 4], [1, 128]]),
                                 PPt.v(0, 128, gcol + 4 * q, [[1, 4], [0, 128]]), ALU.mult),
                              [PB[b], PPt.b], [hT.b])

            def load_slab(fr_tile, wap, r0, nk, c0, ncols):
                src = wap[r0:r0 + nk * 128, c0:c0 + ncols].rearrange("(kc p) c -> p kc c", p=128)
                P.dma("pool", fr_tile.v(0, 128, 0, [[ncols, nk], [1, ncols]]), src, fr_tile.b, db("w"))

            def phase_P(l, hT):
                pb = l * NPP
                wl = w_in[l]
                NTG = TL // 512
                with Frame(P) as fr:
                    wsl = [fr.tile("wsl", [128, 16, 512], BF16) for _ in range(2)]
                    zrow = [fr.tile("zrow", [128, TL], F32) for _ in range(2)]
                    racc = fr.tile("racc", [128, TL], F32)
                    rC = fr.tile("rC", [128, TL], F32)
                    rout = [fr.tile("rout", [128, TL], F32) for _ in range(2)]
                    routb = [fr.tile("routb", [128, TL], BF16) for _ in range(2)]
                    state = {"slab": 0, "bank": 0}

                    def fm_rows(c0, nrows, consume):
                        for s0 in range(0, nrows, 4):
                            W = wsl[state["slab"] % 2]
                            state["slab"] += 1
                            load_slab(W, wl, 0, 16, c0 + s0 * 128, 512)
                            for cg in range(4):
                                ri = s0 + cg
                                for tg in range(NTG):
                                    b = state["bank"] % 4
                                    state["bank"] += 1
                                    for kc in range(16):
                                        P.pe(MM(psv(b, 0, 128, 0, [[1, 512]]),
                                                W.v(0, 128, kc * 512 + cg * 128, [[1, 128]]),
                                                hT.v(0, 128, kc * TL + tg * 512, [[1, 512]]),
                                                start=(kc == 0), stop=(kc == 15)),
                                             [W.b, hT.b], [PB[b]])
                                    consume(ri, tg, b)
                                consume(ri, None, None)

                    def norm_sums(src, scale, bias):
                        for tg in range(NTG):
                            b = 4 + tg
                            P.pe(MM(psv(b, 0, 128, 0, [[1, 512]]), ones_f, src.v(0, 128, tg * 512, [[1, 512]])),
                                 [src.b, CF.b], [PB[b]])
                            P.act(ACT(rC.v(0, 128, tg * 512, [[1, 512]]), psv(b, 0, 128, 0, [[1, 512]]), AF.Ln,
                                      bias=bias, scale=scale), [PB[b]], [rC.b])

                    def c_gdn(ri, tg, b):
                        kind, h = ri // 8, ri % 8
                        R = rout[ri % 2]
                        if tg is not None:
                            if (ri + tg) % 2 == 0:
                                P.act(ACT(R.v(0, 128, tg * 512, [[1, 512]]), psv(b, 0, 128, 0, [[1, 512]]), AF.Copy), [PB[b]], [R.b])
                            else:
                                P.dve(CP(R.v(0, 128, tg * 512, [[1, 512]]), psv(b, 0, 128, 0, [[1, 512]])), [PB[b]], [R.b])
                            return
                        P.dma("sp", ZL[kind][h * 128:(h + 1) * 128, :], R.full(), db(f"ZL{kind}"), R.b)

                    fm_rows(0, 24, c_gdn)

                    def c_gate(ri, tg, b):
                        R = rout[ri % 2]
                        if tg is not None:
                            P.act(ACT(R.v(0, 128, tg * 512, [[1, 512]]), psv(b, 0, 128, 0, [[1, 512]]), AF.Silu), [PB[b]], [R.b])
                            return
                        P.dma("sp", ZL[3][ri * 128:(ri + 1) * 128, :], R.full(), db("ZL3"), R.b)

                    fm_rows(3088, 8, c_gate)

                    def c_u(ri, tg, b):
                        R = rout[ri % 2]
                        if tg is not None:
                            P.act(ACT(R.v(0, 128, tg * 512, [[1, 512]]), psv(b, 0, 128, 0, [[1, 512]]), AF.Gelu), [PB[b]], [R.b])
                            return
                        P.dma("sp", UT[ri], R.full(), db("UT"), R.b)

                    fm_rows(4112, 8, c_u)

                    def c_sqk(ri, tg, b):
                        kind, h = ri // 8, ri % 8
                        Z = zrow[ri % 2]
                        RB = routb[ri % 2]
                        if tg is not None:
                            P.act(ACT(Z.v(0, 128, tg * 512, [[1, 512]]), psv(b, 0, 128, 0, [[1, 512]]), AF.Copy), [PB[b]], [Z.b])
                            return
                        P.act(ACT(racc.full(), Z.full(), AF.Square), [Z.b], [racc.b])
                        norm_sums(racc, 1.0 / 128.0, EPSc)
                        P.act(ACT(rC.full(), rC.full(), AF.Exp, scale=-0.5), [rC.b], [rC.b])
                        gcol = SQS.v(0, 128, l, [[1, 1]]) if kind == 0 else PPt.v(0, 128, pb + 130, [[1, 1]])
                        P.dve(STT(RB.full(), Z.full(), gcol, rC.full(), ALU.mult, ALU.mult), [Z.b, rC.b, SQS.b, PPt.b], [RB.b])
                        P.dma("sp", SQL[kind][h * 128:(h + 1) * 128, :], RB.full(), db(f"SQL{kind}"), RB.b)

                    fm_rows(6160, 16, c_sqk)


                with Frame(P) as fr:
                    NCH = TL // 64
                    wab = fr.tile("wab", [128, 16, 16], BF16)
                    load_slab(wab, wl, 0, 16, 3072, 16)
                    tmp8 = fr.tile("tmp8", [64, 16], F32)
                    gbt = fr.tile("gbt", [64, NCH * 16], F32)
                    for n in range(NCH):
                        b = n % 4
                        for kc in range(16):
                            P.pe(MM(psv(b, 0, 64, 0, [[1, 16]]), hT.v(0, 128, kc * TL + n * 64, [[1, 64]]),
                                    wab.v(0, 128, kc * 16, [[1, 16]]), start=(kc == 0), stop=(kc == 15)),
                                 [hT.b, wab.b], [PB[b]])
                        P.dve(TT(tmp8.v(0, 64, 0, [[1, 8]]), psv(b, 0, 64, 0, [[1, 8]]), ABt.v(0, 64, l * 16 + 8, [[1, 8]]), ALU.add),
                              [PB[b], ABt.b], [tmp8.b])
                        P.act(ACT(tmp8.v(0, 64, 0, [[1, 8]]), tmp8.v(0, 64, 0, [[1, 8]]), AF.Exp), [tmp8.b], [tmp8.b])
                        P.act(ACT(tmp8.v(0, 64, 0, [[1, 8]]), tmp8.v(0, 64, 0, [[1, 8]]), AF.Ln, bias=1.0), [tmp8.b], [tmp8.b])
                        P.dve(TT(gbt.v(0, 64, n * 16, [[8, 2], [1, 4]]), tmp8.v(0, 64, 0, [[4, 2], [1, 4]]),
                                 NEA.v(0, 64, l * 8, [[4, 2], [1, 4]]), ALU.mult), [tmp8.b, NEA.b], [gbt.b])
                        P.act(ACT(gbt.v(0, 64, n * 16 + 4, [[8, 2], [1, 4]]), psv(b, 0, 64, 8, [[4, 2], [1, 4]]), AF.Sigmoid), [PB[b]], [gbt.b])
                    for hg in range(2):
                        P.dma("sp", GBL[hg * TL:(hg + 1) * TL, :].rearrange("(n c) k -> c n k", c=64),
                              gbt.v(0, 64, hg * 8, [[16, NCH], [1, 8]]), db("GBL"), gbt.b)

                    wA = fr.tile("wA", [128, 16, 512], BF16)
                    wB = fr.tile("wB", [128, 16, 512], BF16)
                    wC = fr.tile("wC", [128, 16, 512], BF16)
                    NTI = TL // 128

                    def tm_tile(tile_i, slabs, banks):
                        for si, W in enumerate(slabs):
                            b = banks[si]
                            for kc in range(16):
                                P.pe(MM(psv(b, 0, 128, 0, [[1, 512]]), hT.v(0, 128, kc * TL + tile_i * 128, [[1, 128]]),
                                        W.v(0, 128, kc * 512, [[1, 512]]), start=(kc == 0), stop=(kc == 15)),
                                     [hT.b, W.b], [PB[b]])

                    load_slab(wA, wl, 0, 16, 8208, 512)
                    load_slab(wB, wl, 0, 16, 8208 + 512, 512)
                    vt = [fr.tile("vt", [128, 1024], BF16) for _ in range(2)]
                    for ti in range(NTI):
                        bk = [0, 1] if ti % 2 == 0 else [2, 3]
                        tm_tile(ti, [wA, wB], bk)
                        V_ = vt[ti % 2]
                        for si in range(2):
                            P.act(ACT(V_.v(0, 128, si * 512, [[1, 512]]), psv(bk[si], 0, 128, 0, [[1, 512]]), AF.Copy),
                                  [PB[bk[si]]], [V_.b])
                        for hg in range(2):
                            P.dma("sp", SVL[hg * TL + ti * 128:hg * TL + (ti + 1) * 128, :], V_.v(0, 128, hg * 512, [[1, 512]]),
                                  db("SVL"), V_.b)

                    load_slab(wA, wl, 0, 16, 5136, 512)
                    load_slab(wC, wl, 0, 16, 5136 + 512, 512)
                    lng = fr.tile("lng", [128, 1024], F32)
                    P.dma("sp", lng.full(), bass.AP(lng_in.tensor, l * 1024, [[0, 128], [1, 1024]]), lng.b, db("cst"))
                    wsf = fr.tile("wsf", [128, 1024], F32)
                    P.dma("sp", wsf.full(), wsT_in[l], wsf.b, db("cst"))
                    wsb = fr.tile("wsb", [128, 1024], BF16)
                    P.dve(CP(wsb.full(), wsf.full()), [wsf.b], [wsb.b])
                    P.dve(lambda e, a=wsb.v(64, 64, 0, [[128, 8], [1, 64]]): e.memset(a, 0.0), [wsb.b], [wsb.b])
                    bspb = fr.tile("bspb", [1, 1024], BF16)
                    P.dma("pool", bspb.full(), bass.AP(bsp_in.tensor, l * 1024, [[0, 1], [1, 1024]]), bspb.b, db("cst"))
                    vg = fr.tile("vg", [128, 1024], F32)
                    vb = fr.tile("vb", [128, 1024], BF16)
                    bst = fr.tile("bst", [128, 16], F32)
                    ut = [fr.tile("ut", [128, 8, 128], F32) for _ in range(2)]
                    ubt = [fr.tile("ubt", [128, 8, 128], BF16) for _ in range(2)]
                    for ti in range(NTI):
                        U_ = ut[ti % 2]
                        UB_ = ubt[ti % 2]
                        P.dma("sp", U_.v(0, 128, 0, [[128, 8], [1, 128]]),
                              UT[:, :, ti * 128:(ti + 1) * 128].rearrange("g c t -> c g t"), U_.b, db("UT"))
                        bk = [0, 1]
                        tm_tile(ti, [wA, wC], bk)
                        for si in range(2):
                            P.act(ACT(vg.v(0, 128, si * 512, [[1, 512]]), psv(bk[si], 0, 128, 0, [[1, 512]]), AF.Gelu),
                                  [PB[bk[si]]], [vg.b])
                        for si in range(2):
                            P.dve(lambda e, o=bst.v(0, 128, si * 6, [[1, 6]]), i_=vg.v(0, 128, si * 512, [[1, 512]]): e.bn_stats(o, i_),
                                  [vg.b], [bst.b])
                        P.dve(lambda e, o=bst.v(0, 128, 12, [[1, 2]]), i_=bst.v(0, 128, 0, [[1, 12]]): e.bn_aggr(o, i_), [bst.b], [bst.b])
                        P.act(ACT(bst.v(0, 128, 14, [[1, 1]]), bst.v(0, 128, 13, [[1, 1]]), AF.Ln, bias=EPSc), [bst.b], [bst.b])
                        P.act(ACT(bst.v(0, 128, 15, [[1, 1]]), bst.v(0, 128, 14, [[1, 1]]), AF.Exp, scale=-0.5), [bst.b], [bst.b])
                        P.dve(TS(vg.full(), vg.full(), bst.v(0, 128, 12, [[1, 1]]), bst.v(0, 128, 15, [[1, 1]]), ALU.subtract, ALU.mult),
                              [vg.b, bst.b], [vg.b])
                        P.dve(TT(vb.full(), vg.full(), lng.full(), ALU.mult), [vg.b, lng.b], [vb.b])
                        for g in range(8):
                            b = 2 + g // 4
                            o = psv(b, 0, 128, (g % 4) * 128, [[1, 128]])
                            P.pe(MM(o, vb.v(0, 128, g * 128, [[1, 128]]), wsb.v(0, 128, g * 128, [[1, 128]]), start=True, stop=False),
                                 [vb.b, wsb.b], [PB[b]])
                            P.pe(MM(o, CBt.v(0, 1, CB_ONE, [[1, 128]]), bspb.v(0, 1, g * 128, [[1, 128]]), start=False, stop=True),
                                 [CBt.b, bspb.b], [PB[b]])
                        for q in range(2):
                            P.dve(TT(UB_.v(0, 128, q * 512, [[1, 512]]), U_.v(0, 128, q * 512, [[1, 512]]),
                                     psv(2 + q, 0, 128, 0, [[1, 512]]), ALU.mult), [U_.b, PB[2 + q]], [UB_.b])
                        P.dma("sp", UBT[:, :, ti * 128:(ti + 1) * 128].rearrange("g c t -> c g t"),
                              UB_.v(0, 128, 0, [[128, 8], [1, 128]]), db("UBT"), UB_.b)

            def gates_stream(l, hT, fr):
                wl = w_in[l]
                wh = [fr.tile("wh", [128, 16, 256], BF16) for _ in range(2)]
                rb = [fr.tile("rbg", [128, TL], BF16) for _ in range(2)]
                yield
                k = 0
                for hs in range(24):
                    W = wh[hs % 2]
                    load_slab(W, wl, 0, 16, 9232 + hs * 256, 256)
                    for cg in range(2):
                        ri = hs * 2 + cg
                        RB = rb[ri % 2]
                        for tg in range(TL // 512):
                            b = 4 + k % 4
                            k += 1
                            for kc in range(16):
                                P.pe(MM(psv(b, 0, 128, 0, [[1, 512]]), W.v(0, 128, kc * 256 + cg * 128, [[1, 128]]),
                                        hT.v(0, 128, kc * TL + tg * 512, [[1, 512]]), start=(kc == 0), stop=(kc == 15)),
                                     [W.b, hT.b], [PB[b]])
                                if kc % 2 == 1:
                                    yield
                            P.act(ACT(RB.v(0, 128, tg * 512, [[1, 512]]), psv(b, 0, 128, 0, [[1, 512]]), AF.Sigmoid), [PB[b]], [RB.b])
                        P.dma("sp", GATES[ri], RB.full(), db("GATES"), RB.b)

            def exchange(local, send, gath, full, name, n, part):
                if part in (0, 2):
                    P.dma("sp", send, local[bass.ts(other, n)], db("S" + name), db(name))
                    P.cc(send, gath, db("G" + name), db("S" + name), groups)
                    P.dma("sp", full[bass.ts(rank, n)], local[bass.ts(rank, n)], db("F" + name), db(name))
                if part in (1, 2):
                    P.dma("sp", full[bass.ts(other, n)], gath[bass.ts(other, n)], db("F" + name), db("G" + name))

            def phase_X1(part):
                for k in range(4):
                    exchange(ZL[k], SENDZ[k], GZ[k], FZ[k], f"ZL{k}", 512, part)
                    yield
                for k in range(2):
                    exchange(SQL[k], SENDQ[k], GQ[k], FSQ[k], f"SQL{k}", 512, part)
                    yield
                exchange(SVL, SENDV, GV, FSV, "SVL", TL, part)
                yield
                exchange(GBL, SENDG, GGB, FGB, "GBL", TL, part)
                yield

            def phase_X2():
                exchange(OAm, SENDO[0], GO[0], BRA, "OAm", 512, 2)
                exchange(OCm, SENDO[1], GO[1], BRC, "OCm", 512, 2)

            def phase_G0(l):
                with Frame(P) as fr:
                    cwm = fr.tile("cwm", [128, 48], F32)
                    P.dma("sp", cwm.full(), cw_in[bass.ts(rank + 2 * l, 128)], cwm.b, db("cst"))
                    DG = fr.tile("DG", [128, 48, 128], BF16)
                    for c in range(48):
                        P.dve(TS(DG.v(0, 128, c * 128, [[1, 128]]), ident, cwm.v(0, 128, c, [[1, 1]]), None, ALU.mult), [CF.b, cwm.b], [DG.b])
                    zb = [fr.tile("zb", [128, T + 4], BF16) for _ in range(2)]
                    rB = fr.tile("rB", [128, T], F32)
                    rsq = fr.tile("rsq", [128, T], BF16)
                    rC = fr.tile("rC", [128, T], F32)
                    rout = [fr.tile("rout", [128, T], F32) for _ in range(2)]
                    for z in zb:
                        P.dve(lambda e, a=z.v(0, 128, 0, [[1, 4]]): e.memset(a, 0.0), [], [z.b])
                    i = 0
                    for kind in range(3):
                        for hh in range(4):
                            Z, R = zb[i % 2], rout[i % 2]
                            i += 1
                            for half in range(2):
                                P.dma("pool", Z.v(0, 128, 4 + half * TL, [[1, TL]]),
                                      FZ[kind][half * 512 + hh * 128:half * 512 + (hh + 1) * 128, :], Z.b, db(f"FZL{kind}"))
                            dst = R if kind == 2 else rB
                            for tg in range(4):
                                b_ = tg
                                for j in range(4):
                                    P.pe(MM(psv(b_, 0, 128, 0, [[1, 512]]), DG.v(0, 128, ((kind * 4 + hh) * 4 + j) * 128, [[1, 128]]),
                                            Z.v(0, 128, 1 + j + tg * 512, [[1, 512]]), start=(j == 0), stop=(j == 3)), [DG.b, Z.b], [PB[b_]])
                                P.act(ACT(dst.v(0, 128, tg * 512, [[1, 512]]), psv(b_, 0, 128, 0, [[1, 512]]), AF.Silu), [PB[b_]], [dst.b])
                            if kind == 2:
                                P.dma("sp", VT[hh], R.full(), db("VT"), R.b)
                                yield
                                continue
                            P.act(ACT(rsq.full(), rB.full(), AF.Square), [rB.b], [rsq.b])
                            for tg in range(4):
                                b_ = 4 + tg
                                P.pe(MM(psv(b_, 0, 128, 0, [[1, 512]]), ones_b, rsq.v(0, 128, tg * 512, [[1, 512]])), [rsq.b, CBt.b], [PB[b_]])
                                P.act(ACT(rC.v(0, 128, tg * 512, [[1, 512]]), psv(b_, 0, 128, 0, [[1, 512]]), AF.Ln, bias=EPSc), [PB[b_]], [rC.b])
                            P.act(ACT(rC.full(), rC.full(), AF.Exp, scale=-0.5), [rC.b], [rC.b])
                            sc = float(128 ** -0.5) if kind == 0 else 1.0
                            P.dve(STT(R.full(), rB.full(), sc, rC.full(), ALU.mult, ALU.mult), [rB.b, rC.b], [R.b])
                            P.dma("sp", (QT if kind == 0 else KT)[hh], R.full(), db("QT" if kind == 0 else "KT"), R.b)
                            yield

            def phase_G(l):
                pb = l * NPP
                gng = PPt.v(0, 128, pb + 128, [[1, 1]])
                with Frame(P) as fr:
                    GBs = fr.tile("GBs", [64, 32 * 8], F32)
                    P.dma("sp", GBs.v(0, 64, 0, [[8, 32], [1, 8]]), FGB.rearrange("(n c) k -> c n k", c=64), GBs.b, db("FGBL"))
                    qS = fr.tile("qS", [128, 4, 512], F32)
                    kS = fr.tile("kS", [128, 4, 512], F32)
                    vS = fr.tile("vS", [128, 4, 512], F32)
                    gS = fr.tile("gS", [128, 4, 512], F32)
                    oaS = [fr.tile("oaS", [128, 4, 512], BF16) for _ in range(2)]
                    Sst = [fr.tile("S", [128, 4, 128], F32) for _ in range(2)]
                    gbc = fr.tile("gbc", [64, 16], F32)
                    gl = fr.tile("gl", [64, 8, 128], F32)
                    GbS = fr.tile("GbS", [128, 8, 64], F32)
                    sm = fr.tile("sm", [64, 48], F32)
                    t1 = fr.tile("t1", [64, 8, 64], F32)
                    t2 = fr.tile("t2", [64, 8, 64], F32)
                    dec = fr.tile("dec", [64, 8, 64], F32)
                    decT = fr.tile("decT", [64, 8, 64], F32)
                    rhsW = fr.tile("rhsW", [64, 8, 128], BF16)
                    rhsV = fr.tile("rhsV", [64, 8, 128], BF16)
                    kSb = fr.tile("kSb", [128, 4, 512], BF16)
                    qSb = fr.tile("qSb", [128, 4, 512], BF16)
                    Mb = fr.tile("Mb", [64, 8, 64], F32)
                    A0 = fr.tile("A0", [64, 8, 64], F32)
                    Rt = [fr.tile("R", [64, 8, 64], F32) for _ in range(2)]
                    PX = [fr.tile("PX", [64, 8, 128], F32) for _ in range(2)]
                    XSt = fr.tile("XSt", [64, 8, 64], BF16)
                    eGb2 = [fr.tile("eGb", [128, 8, 64], F32) for _ in range(2)]
                    kdec2 = [fr.tile("kdec", [64, 8, 128], BF16) for _ in range(2)]
                    uS2 = [fr.tile("uS", [64, 8, 128], F32) for _ in range(2)]
                    wTS2 = [fr.tile("wTS", [128, 8, 64], F32) for _ in range(2)]
                    intraT2 = [fr.tile("intraT", [64, 8, 64], BF16) for _ in range(2)]
                    qdT2 = [fr.tile("qdT", [128, 8, 64], F32) for _ in range(2)]
                    vnew = [fr.tile("vnew", [64, 4, 128], BF16) for _ in range(2)]
                    oS = [fr.tile("oS", [128, 4, 64], F32) for _ in range(2)]
                    o2 = [fr.tile("o2", [128, 4, 64], F32) for _ in range(2)]
                    rs = [fr.tile("rs", [128, 4, 64], F32) for _ in range(2)]
                    P.dve(lambda e, a=Sst[0].full(): e.memset(a, 0.0), [], [Sst[0].b])
                    id64 = CF.v(0, 64, C_ID, [[1, 64]])
                    triU64 = CF.v(0, 64, C_TRIU, [[1, 64]])

                    def pre(pi):
                        n0 = 2 * pi
                        sg, co = n0 // 8, (n0 % 8) * 64
                        eGb, kdec, uS, wTS, intraT, qdT = (eGb2[pi % 2], kdec2[pi % 2], uS2[pi % 2], wTS2[pi % 2],
                                                           intraT2[pi % 2], qdT2[pi % 2])
                        if n0 % 8 == 0:
                            for (Tt, Dr, nm) in ((qS, QT, "QT"), (kS, KT, "KT"), (vS, VT, "VT")):
                                P.dma("sp", Tt.v(0, 128, 0, [[512, 4], [1, 512]]),
                                      Dr[:, :, sg * 512:(sg + 1) * 512].rearrange("h d t -> d h t"), Tt.b, db(nm))
                            P.pool(CP(kSb.full(), kS.full()), [kS.b], [kSb.b])
                            P.pool(CP(qSb.full(), qS.full()), [qS.b], [qSb.b])

                        def uview(Tt, u, ncol=64):
                            cb, hh = divmod(u, 4)
                            return Tt.v(0, 128, hh * 512 + co + cb * 64, [[1, ncol]])

                        P.dve(CP(gbc.v(0, 64, 0, [[4, 2], [1, 4]]), GBs.v(0, 64, n0 * 8, [[8, 2], [1, 4]])), [GBs.b], [gbc.b])
                        P.dve(CP(gbc.v(0, 64, 8, [[4, 2], [1, 4]]), GBs.v(0, 64, n0 * 8 + 4, [[8, 2], [1, 4]])), [GBs.b], [gbc.b])
                        graw = gbc.v(0, 64, 0, [[1, 8]])
                        beta_bs = lambda w: gbc.v(0, 64, 8, [[1, 8], [0, w]])
                        P.dve(CP(gl.v(0, 64, 0, [[128, 8], [1, 128]]), gbc.v(0, 64, 0, [[1, 8], [0, 128]])), [gbc.b], [gl.b])
                        for u in range(8):
                            P.pe(MM(psv(2, 0, 128, u * 64, [[1, 64]]), gl.v(0, 64, u * 128, [[1, 128]]), triU64), [gl.b, CF.b], [PB[2]])
                        P.pe(MM(psv(0, 0, 64, 0, [[1, 8]]), triU64, graw), [gbc.b, CF.b], [PB[0]])
                        yield
                        P.act(ACT(GbS.full(), psv(2, 0, 128, 0, [[1, 512]]), AF.Copy), [PB[2]], [GbS.b])
                        P.act(ACT(eGb.full(), psv(2, 0, 128, 0, [[1, 512]]), AF.Exp), [PB[2]], [eGb.b])
                        P.dve(CP(sm.v(0, 64, 0, [[1, 8]]), psv(0, 0, 64, 0, [[1, 8]])), [PB[0]], [sm.b])
                        Gc_bs = sm.v(0, 64, 0, [[1, 8], [0, 64]])
                        P.dve(STT(t1.v(0, 64, 0, [[64, 8], [1, 64]]), GbS.v(0, 64, 0, [[64, 8], [1, 64]]), -1.0, Gc_bs, ALU.mult, ALU.add),
                              [GbS.b, sm.b], [t1.b])
                        P.pool(TT(t1.v(0, 64, 0, [[64, 8], [1, 64]]), t1.v(0, 64, 0, [[64, 8], [1, 64]]),
                                  CF.v(0, 64, C_MBL, [[0, 8], [1, 64]]), ALU.add), [t1.b, CF.b], [t1.b])
                        P.act(ACT(dec.full(), t1.full(), AF.Exp), [t1.b], [dec.b])
                        yield
                        P.dve(TT(t2.v(0, 64, 0, [[64, 8], [1, 64]]), GbS.v(0, 64, 0, [[64, 8], [1, 64]]), Gc_bs, ALU.subtract),
                              [GbS.b, sm.b], [t2.b])
                        P.pool(TT(t2.v(0, 64, 0, [[64, 8], [1, 64]]), t2.v(0, 64, 0, [[64, 8], [1, 64]]),
                                  CF.v(0, 64, C_MBU, [[0, 8], [1, 64]]), ALU.add), [t2.b, CF.b], [t2.b])
                        P.act(ACT(decT.full(), t2.full(), AF.Exp), [t2.b], [decT.b])
                        P.dve(TT(sm.v(0, 64, 8, [[1, 8]]), GbS.v(0, 64, 63, [[64, 8]]), sm.v(0, 64, 0, [[1, 8]]), ALU.subtract),
                              [GbS.b, sm.b], [sm.b])
                        P.act(ACT(sm.v(0, 64, 8, [[1, 8]]), sm.v(0, 64, 8, [[1, 8]]), AF.Exp), [sm.b], [sm.b])
                        P.act(ACT(sm.v(0, 64, 16, [[1, 8]]), sm.v(0, 64, 0, [[1, 8]]), AF.Exp), [sm.b], [sm.b])
                        P.dve(TT(sm.v(0, 64, 24, [[1, 8]]), sm.v(0, 64, 16, [[1, 8]]), gbc.v(0, 64, 8, [[1, 8]]), ALU.mult),
                              [sm.b, gbc.b], [sm.b])
                        yield
                        for u in range(8):
                            P.pe(TR(psv(0, 0, 64, u * 128, [[1, 128]]), uview(kS, u), ident), [kS.b, CF.b], [PB[0], PB[1]])
                        for u in range(8):
                            P.pe(TR(psv(2, 0, 64, u * 128, [[1, 128]]), uview(vS, u), ident), [vS.b, CF.b], [PB[2], PB[3]])
                        yield
                        psK = psv(0, 0, 64, 0, [[128, 8], [1, 128]])
                        psV = psv(2, 0, 64, 0, [[128, 8], [1, 128]])
                        P.dve(TT(rhsW.v(0, 64, 0, [[128, 8], [1, 128]]), psK, sm.v(0, 64, 24, [[1, 8], [0, 128]]), ALU.mult),
                              [PB[0], PB[1], sm.b], [rhsW.b])
                        P.dve(TT(kdec.v(0, 64, 0, [[128, 8], [1, 128]]), psK, sm.v(0, 64, 8, [[1, 8], [0, 128]]), ALU.mult),
                              [PB[0], PB[1], sm.b], [kdec.b])
                        P.dve(TT(rhsV.v(0, 64, 0, [[128, 8], [1, 128]]), psV, beta_bs(128), ALU.mult), [PB[2], PB[3], gbc.b], [rhsV.b])
                        yield
                        for u in range(8):
                            kc_ = uview(kSb, u)
                            P.pe(MM(psv(2, 0, 64, u * 64, [[1, 64]]), kc_, kc_), [kSb.b], [PB[2]])
                        P.pool(TT(Mb.v(0, 64, 0, [[64, 8], [1, 64]]), CF.v(0, 64, C_NSTR, [[0, 8], [1, 64]]), beta_bs(64), ALU.mult),
                               [CF.b, gbc.b], [Mb.b])
                        yield
                        P.dve(TT(A0.full(), psv(2, 0, 64, 0, [[1, 512]]), dec.full(), ALU.mult), [PB[2], dec.b], [A0.b])
                        R0 = Rt[0]
                        P.dve(TT(R0.full(), A0.full(), Mb.full(), ALU.mult), [A0.b, Mb.b], [R0.b])
                        for u in range(8):
                            P.pe(TR(psv(3, 0, 64, u * 64, [[1, 64]]), R0.v(0, 64, u * 64, [[1, 64]]), id64), [R0.b, CF.b], [PB[3]])
                        yield
                        P.act(ACT(PX[0].v(0, 64, 0, [[128, 8], [1, 64]]), psv(3, 0, 64, 0, [[64, 8], [1, 64]]), AF.Copy), [PB[3]], [PX[0].b])
                        P.pool(CP(PX[0].v(0, 64, 64, [[128, 8], [1, 64]]), CF.v(0, 64, C_ID, [[0, 8], [1, 64]])), [CF.b], [PX[0].b])
                        for j in range(6):
                            src, dst = PX[j % 2], PX[(j + 1) % 2]
                            Rs, Rd = Rt[j % 2], Rt[(j + 1) % 2]
                            last = j == 5
                            c0 = 64 if last else 0
                            for u in range(8):
                                P.pe(MM(psv(0, 0, 64, u * 128 + c0, [[1, 128 - c0]]), Rs.v(0, 64, u * 64, [[1, 64]]),
                                        src.v(0, 64, u * 128 + c0, [[1, 128 - c0]])), [Rs.b, src.b], [PB[0], PB[1]])
                            if not last:
                                for u in range(8):
                                    P.pe(MM(psv(2, 0, 64, u * 64, [[1, 64]]), src.v(0, 64, u * 128, [[1, 64]]),
                                            Rs.v(0, 64, u * 64, [[1, 64]])), [Rs.b, src.b], [PB[2]])
                                yield
                                P.act(ACT(dst.v(0, 64, 0, [[128, 8], [1, 64]]), psv(0, 0, 64, 0, [[128, 8], [1, 64]]), AF.Copy),
                                      [PB[0], PB[1]], [dst.b])
                                P.dve(TT(dst.v(0, 64, 64, [[128, 8], [1, 64]]), src.v(0, 64, 64, [[128, 8], [1, 64]]),
                                         psv(0, 0, 64, 64, [[128, 8], [1, 64]]), ALU.add), [src.b, PB[0], PB[1]], [dst.b])
                                P.act(ACT(Rd.full(), psv(2, 0, 64, 0, [[1, 512]]), AF.Copy), [PB[2]], [Rd.b])
                                yield
                            else:
                                yield
                                P.dve(TT(XSt.v(0, 64, 0, [[64, 8], [1, 64]]), src.v(0, 64, 64, [[128, 8], [1, 64]]),
                                         psv(0, 0, 64, 64, [[128, 8], [1, 64]]), ALU.add), [src.b, PB[0], PB[1]], [XSt.b])
                                yield
                        for u in range(8):
                            P.pe(MM(psv(2, 0, 64, u * 128, [[1, 128]]), XSt.v(0, 64, u * 64, [[1, 64]]), rhsV.v(0, 64, u * 128, [[1, 128]])),
                                 [XSt.b, rhsV.b], [PB[2], PB[3]])
                        for u in range(8):
                            P.pe(MM(psv(0, 0, 128, u * 64, [[1, 64]]), rhsW.v(0, 64, u * 128, [[1, 128]]), XSt.v(0, 64, u * 64, [[1, 64]])),
                                 [XSt.b, rhsW.b], [PB[0]])
                        yield
                        P.act(ACT(uS.full(), psv(2, 0, 64, 0, [[1, 1024]]), AF.Copy), [PB[2], PB[3]], [uS.b])
                        P.act(ACT(wTS.full(), psv(0, 0, 128, 0, [[1, 512]]), AF.Copy), [PB[0]], [wTS.b])
                        for u in range(8):
                            P.pe(MM(psv(1, 0, 64, u * 64, [[1, 64]]), uview(kSb, u), uview(qSb, u)), [kSb.b, qSb.b], [PB[1]])
                        yield
                        P.dve(TT(intraT.full(), psv(1, 0, 64, 0, [[1, 512]]), decT.full(), ALU.mult), [PB[1], decT.b], [intraT.b])
                        P.pool(TT(qdT.v(0, 128, 0, [[256, 2], [64, 4], [1, 64]]), qS.v(0, 128, co, [[64, 2], [512, 4], [1, 64]]),
                                  eGb.v(0, 128, 0, [[256, 2], [64, 4], [1, 64]]), ALU.mult), [qS.b, eGb.b], [qdT.b])
                        yield

                    sidx = [0]

                    def rec(pi):
                        n0 = 2 * pi
                        sg, co = n0 // 8, (n0 % 8) * 64
                        OA = oaS[sg % 2]
                        eGb, kdec, uS, wTS, intraT, qdT = (eGb2[pi % 2], kdec2[pi % 2], uS2[pi % 2], wTS2[pi % 2],
                                                           intraT2[pi % 2], qdT2[pi % 2])
                        if n0 % 8 == 0:
                            hf, tc0 = sg // 2, (sg % 2) * 512
                            P.dma("sp", gS.v(0, 128, 0, [[512, 4], [1, 512]]),
                                  FZ[3][hf * 512:(hf + 1) * 512, tc0:tc0 + 512].rearrange("(h d) t -> d h t", d=128), gS.b, db("FZL3"))
                        for cb in range(2):
                            u0 = cb * 4
                            Scur, Snxt = Sst[sidx[0] % 2], Sst[(sidx[0] + 1) % 2]
                            sidx[0] += 1
                            VN, OS_, O2_, RS_ = vnew[cb], oS[cb], o2[cb], rs[cb]
                            bWS, bO, bS, bN = 4, 5, 4, 5
                            for hh in range(4):
                                P.pe(MM(psv(bWS, 0, 64, hh * 128, [[1, 128]]), wTS.v(0, 128, (u0 + hh) * 64, [[1, 64]]),
                                        Scur.v(0, 128, hh * 128, [[1, 128]])), [wTS.b, Scur.b], [PB[bWS]])
                            yield
                            P.dve(TT(VN.full(), uS.v(0, 64, u0 * 128, [[1, 512]]), psv(bWS, 0, 64, 0, [[1, 512]]), ALU.subtract),
                                  [uS.b, PB[bWS]], [VN.b])
                            yield
                            for hh in range(4):
                                P.pe(MM(psv(bS, 0, 128, hh * 128, [[1, 128]]), kdec.v(0, 64, (u0 + hh) * 128, [[1, 128]]),
                                        VN.v(0, 64, hh * 128, [[1, 128]])), [kdec.b, VN.b], [PB[bS]])
                            for hh in range(4):
                                o = psv(bO, 0, 128, hh * 64, [[1, 64]])
                                P.pe(MM(o, Scur.v(0, 128, hh * 128, [[1, 128]]), qdT.v(0, 128, (u0 + hh) * 64, [[1, 64]]), start=True, stop=False),
                                     [Scur.b, qdT.b], [PB[bO]])
                                P.pe(MM(o, VN.v(0, 64, hh * 128, [[1, 128]]), intraT.v(0, 64, (u0 + hh) * 64, [[1, 64]]), start=False, stop=True),
                                     [VN.b, intraT.b], [PB[bO]])
                            P.dve(TT(Snxt.v(0, 128, 0, [[128, 4], [1, 128]]), Scur.v(0, 128, 0, [[128, 4], [1, 128]]),
                                     eGb.v(0, 128, u0 * 64 + 63, [[64, 4], [0, 128]]), ALU.mult), [Scur.b, eGb.b], [Snxt.b])
                            yield
                            P.dve(TT(Snxt.full(), Snxt.full(), psv(bS, 0, 128, 0, [[1, 512]]), ALU.add), [Snxt.b, PB[bS]], [Snxt.b])
                            P.act(ACT(OS_.full(), psv(bO, 0, 128, 0, [[1, 256]]), AF.Copy), [PB[bO]], [OS_.b])
                            P.act(ACT(O2_.full(), psv(bO, 0, 128, 0, [[1, 256]]), AF.Square), [PB[bO]], [O2_.b])
                            P.pe(MM(psv(bN, 0, 128, 0, [[1, 256]]), ones_f, O2_.full()), [O2_.b, CF.b], [PB[bN]])
                            yield
                            P.act(ACT(RS_.full(), psv(bN, 0, 128, 0, [[1, 256]]), AF.Ln, bias=EPSc, scale=1.0 / 128.0), [PB[bN]], [RS_.b])
                            P.act(ACT(RS_.full(), RS_.full(), AF.Exp, scale=-0.5), [RS_.b], [RS_.b])
                            P.dve(TT(OS_.full(), OS_.full(), RS_.full(), ALU.mult), [OS_.b, RS_.b], [OS_.b])
                            cc_ = co + cb * 64
                            P.dve(STT(OA.v(0, 128, cc_, [[512, 4], [1, 64]]), OS_.v(0, 128, 0, [[64, 4], [1, 64]]), gng,
                                      gS.v(0, 128, cc_, [[512, 4], [1, 64]]), ALU.mult, ALU.mult), [OS_.b, gS.b, PPt.b], [OA.b])
                            yield
                        if n0 % 8 == 6:
                            hf, tc0 = sg // 2, (sg % 2) * 512
                            P.dma("sp", OAm[hf * 512:(hf + 1) * 512, tc0:tc0 + 512].rearrange("(h d) t -> d h t", d=128),
                                  OA.v(0, 128, 0, [[512, 4], [1, 512]]), db("OAm"), OA.b)

                    for _ in pre(0):
                        pass
                    for pi in range(16):
                        gens = [[rec(pi), 1]] + ([[pre(pi + 1), QA]] if pi < 15 else [])
                        while gens:
                            for ent in list(gens):
                                try:
                                    for _ in range(ent[1]):
                                        next(ent[0])
                                except StopIteration:
                                    gens.remove(ent)
                    yield

            def phase_S(l):
                with Frame(P) as fr:
                    Vall = fr.tile("Vall", [128, 16, 512], BF16)
                    qTh = [fr.tile("qTh", [128, T], BF16) for _ in range(2)]
                    kTh = [fr.tile("kTh", [128, T], BF16) for _ in range(2)]
                    nkTh = [fr.tile("nkTh", [128, T], BF16) for _ in range(2)]
                    Et = [fr.tile("E", [128, 512], F32) for _ in range(3)]
                    spb = [fr.tile("spb", [128, 512], BF16) for _ in range(3)]
                    acc = fr.tile("acc", [128, 512], BF16)
                    AT = [fr.tile("AT", [128, 512], BF16) for _ in range(2)]
                    ocT = [fr.tile("ocT", [128, T], BF16) for _ in range(2)]
                    triS = CBt.v(0, 128, CB_TRIS, [[1, 128]])
                    P.dma("sp", Vall.v(0, 128, 0, [[512, 16], [1, 512]]), FSV.rearrange("(b p) c -> p b c", p=128), Vall.b, db("FSVL"))
                    items = [(h, qg, kb) for h in range(4) for qg in range(4) for kb in range(4 * qg + 3, -1, -1)]

                    def stage1(i):
                        h, qg, kb = items[i]
                        Q, K = qTh[h % 2], kTh[h % 2]
                        if qg == 0 and kb == 3:
                            for half in range(2):
                                r0 = half * 512 + h * 128
                                P.dma("sp", Q.v(0, 128, half * TL, [[1, TL]]), FSQ[0][r0:r0 + 128, :], Q.b, db("FSQL0"))
                                P.dma("sp", K.v(0, 128, half * TL, [[1, TL]]), FSQ[1][r0:r0 + 128, :], K.b, db("FSQL1"))
                            P.pool(TS(nkTh[h % 2].full(), K.full(), -1.0, None, ALU.mult), [K.b], [nkTh[h % 2].b])
                        E_, SP_ = Et[i % 3], spb[i % 3]
                        bZ = (0, 1, 6)[i % 3]
                        P.pe(MM(psv(bZ, 0, 128, 0, [[1, 512]]), K.v(0, 128, kb * 128, [[1, 128]]), Q.v(0, 128, qg * 512, [[1, 512]])),
                             [K.b, Q.b], [PB[bZ]])
                        P.act(ACT(E_.full(), psv(bZ, 0, 128, 0, [[1, 512]]), AF.Exp), [PB[bZ]], [E_.b])
                        P.act(ACT(SP_.full(), E_.full(), AF.Ln, bias=1.0), [E_.b], [SP_.b])
                        di = kb - 4 * qg
                        if di >= 0:
                            mk = CBt.v(0, 128, CB_MASK + di * 512, [[1, 512]])
                            P.dve(TT(SP_.full(), SP_.full(), mk, ALU.mult), [SP_.b, CBt.b], [SP_.b])

                    def stage2(i):
                        h, qg, kb = items[i]
                        Q, NK, OC = qTh[h % 2], nkTh[h % 2], ocT[h % 2]
                        kmax = 4 * qg + 3
                        SP_, A_ = spb[i % 3], AT[i % 2]
                        bC = 2 + i % 2
                        bO = 4 + (h * 4 + qg) % 2
                        qv = Q.v(0, 128, qg * 512, [[1, 512]])
                        oC = psv(bC, 0, 128, 0, [[1, 512]])
                        P.pe(MM(oC, triS, SP_.full(), start=True, stop=False), [SP_.b, CBt.b], [PB[bC]])
                        if kb < kmax:
                            P.pe(MM(oC, ones_b, acc.full(), start=False, stop=False), [acc.b, CBt.b], [PB[bC]])
                        P.pe(MM(oC, NK.v(0, 128, kb * 128, [[1, 128]]), qv, start=False, stop=True), [NK.b, Q.b], [PB[bC]])
                        P.act(ACT(A_.full(), oC, AF.Exp, scale=-1.0), [PB[bC]], [A_.b])
                        di = kb - 4 * qg
                        if di >= 0:
                            mk = CBt.v(0, 128, CB_MASK + di * 512, [[1, 512]])
                            P.dve(TT(A_.full(), A_.full(), mk, ALU.mult), [A_.b, CBt.b], [A_.b])
                        P.pe(MM(psv(bO, 0, 128, 0, [[1, 512]]), Vall.v(0, 128, kb * 512 + h * 128, [[1, 128]]), A_.full(),
                                start=(kb == kmax), stop=(kb == 0)), [Vall.b, A_.b], [PB[bO]])
                        if kb > 0:
                            if kb == kmax:
                                P.pool(CP(acc.full(), SP_.full()), [SP_.b], [acc.b])
                            else:
                                P.pool(TT(acc.full(), acc.full(), SP_.full(), ALU.add), [acc.b, SP_.b], [acc.b])
                        else:
                            P.act(ACT(OC.v(0, 128, qg * 512, [[1, 512]]), psv(bO, 0, 128, 0, [[1, 512]]), AF.Copy), [PB[bO]], [OC.b])
                            if qg == 3:
                                for half in range(2):
                                    r0 = half * 512 + h * 128
                                    P.dma("sp", OCm[r0:r0 + 128, :], OC.v(0, 128, half * TL, [[1, TL]]), db("OCm"), OC.b)

                    n = len(items)
                    SK = 2
                    for i in range(n + SK):
                        if i < n:
                            stage1(i)
                        if i >= SK:
                            stage2(i - SK)

            def phase_O(l, xin, xin_b, xout, xout_b):
                wsrc = (w_oa[l], w_ob[l], w_oc[l])
                with Frame(P) as fo:
                    yT = fo.tile("yT", [128, 16, TL], BF16)
                    wo = [fo.tile("wo", [128, 16, 512], BF16) for _ in range(2)]
                    xp = [fo.tile("xp", [128, 512], F32) for _ in range(2)]
                    xo = [fo.tile("xo", [128, 512], F32) for _ in range(2)]
                    cnt = 0
                    with Frame(P) as fr:
                        br = [fr.tile("br", [128, 8, TL], BF16) for _ in range(3)]
                        wbr = [[fr.tile("wbr", [128, 8, 512], BF16) for _ in range(3)] for _ in range(2)]
                        gsl = [[fr.tile("gsl", [128, TL], BF16) for _ in range(3)] for _ in range(2)]
                        tA = fr.tile("tA", [128, 512], F32)
                        tB = fr.tile("tB", [128, 512], F32)
                        P.dma("sp", br[0].v(0, 128, 0, [[TL, 8], [1, TL]]), BRA.rearrange("(h d) t -> d h t", d=128), br[0].b, db("FOAm"))
                        P.dma("sp", br[1].v(0, 128, 0, [[TL, 8], [1, TL]]), UBT.rearrange("h d t -> d h t"), br[1].b, db("UBT"))
                        P.dma("sp", br[2].v(0, 128, 0, [[TL, 8], [1, TL]]), BRC.rearrange("(h d) t -> d h t", d=128), br[2].b, db("FOCm"))
                        for js in range(4):
                            WB = wbr[js % 2]
                            for i in range(3):
                                load_slab(WB[i], wsrc[i], 0, 8, js * 512, 512)
                            for jj in range(4):
                                j = js * 4 + jj
                                GS = gsl[j % 2]
                                for i in range(3):
                                    P.dma("sp", GS[i].full(), GATES[i * 16 + j], GS[i].b, db("GATES"))
                                for sub in range(TL // 512):
                                    bks = [(cnt * 3 + i) % 6 for i in range(3)]
                                    cnt += 1
                                    for i in range(3):
                                        for kc in range(8):
                                            P.pe(MM(psv(bks[i], 0, 128, 0, [[1, 512]]), WB[i].v(0, 128, kc * 512 + jj * 128, [[1, 128]]),
                                                    br[i].v(0, 128, kc * TL + sub * 512, [[1, 512]]), start=(kc == 0), stop=(kc == 7)),
                                                 [WB[i].b, br[i].b], [PB[bks[i]]])
                                    gv = [GS[i].v(0, 128, sub * 512, [[1, 512]]) for i in range(3)]
                                    P.dve(TT(tA.full(), psv(bks[0], 0, 128, 0, [[1, 512]]), gv[0], ALU.mult), [PB[bks[0]], GS[0].b], [tA.b])
                                    P.dve(TT(tB.full(), psv(bks[1], 0, 128, 0, [[1, 512]]), gv[1], ALU.mult), [PB[bks[1]], GS[1].b], [tB.b])
                                    P.pool(TT(tA.full(), tA.full(), tB.full(), ALU.add), [tA.b, tB.b], [tA.b])
                                    P.dve(TT(tB.full(), psv(bks[2], 0, 128, 0, [[1, 512]]), gv[2], ALU.mult), [PB[bks[2]], GS[2].b], [tB.b])
                                    P.dve(TT(yT.v(0, 128, j * TL + sub * 512, [[1, 512]]), tA.full(), tB.full(), ALU.add), [tA.b, tB.b], [yT.b])
                        load_slab(wo[0], w_o[l], 0, 16, 0, 512)
                        load_slab(wo[1], w_o[l], 0, 16, 512, 512)
                    if True:
                        for cs in range(4):
                            WO = wo[cs % 2]
                            if cs >= 2:
                                load_slab(WO, w_o[l], 0, 16, cs * 512, 512)
                            for ti in range(TL // 128):
                                k = cs * 8 + ti
                                XP, XO = xp[k % 2], xo[k % 2]
                                r0 = ti * 128
                                P.dma("sp", XP.full(), xin[r0:r0 + 128, cs * 512:(cs + 1) * 512], XP.b, xin_b)
                                b = 6 + k % 2
                                for kc in range(16):
                                    P.pe(MM(psv(b, 0, 128, 0, [[1, 512]]), yT.v(0, 128, kc * TL + ti * 128, [[1, 128]]),
                                            WO.v(0, 128, kc * 512, [[1, 512]]), start=(kc == 0), stop=(kc == 15)), [yT.b, WO.b], [PB[b]])
                                P.dve(TT(XO.full(), psv(b, 0, 128, 0, [[1, 512]]), XP.full(), ALU.add), [PB[b], XP.b], [XO.b])
                                P.dma("sp", xout[r0:r0 + 128, cs * 512:(cs + 1) * 512], XO.full(), xout_b, XO.b)

            def phase_F(l, xa, xa_b, xb, xb_b, xc, xc_b):
                pb = l * NPP
                with Frame(P) as fo:
                    h2T = fo.tile("h2T", [128, 16, TL], BF16)
                    w1 = [fo.tile("w1", [128, 16, 512], BF16) for _ in range(2)]
                    rl = [fo.tile("rl", [128, 512], F32) for _ in range(2)]
                    w2 = [fo.tile("w2", [128, 32, 256], BF16) for _ in range(2)]
                    xp = [fo.tile("xp", [128, 256], F32) for _ in range(2)]
                    xo = [fo.tile("xo", [128, 256], F32) for _ in range(2)]
                    c1 = 0
                    c2 = 0
                    with Frame(P) as fn:
                        norm_T(fn, xa, xa_b, 0, TL // 128, pb + 16, h2T, 0)
                    for half in range(2):
                        src, src_b, dst, dst_b = (xa, xa_b, xb, xb_b) if half == 0 else (xb, xb_b, xc, xc_b)
                        with Frame(P) as fa:
                            aT = fa.tile("aT", [128, 32, TL], BF16)
                            if True:
                                for s_ in range(8):
                                    W1 = w1[(half * 8 + s_) % 2]
                                    load_slab(W1, w_f1[l], 0, 16, half * 4096 + s_ * 512, 512)
                                    for cg in range(4):
                                        fc = s_ * 4 + cg
                                        for sub in range(TL // 512):
                                            b = c1 % 4
                                            RL = rl[c1 % 2]
                                            c1 += 1
                                            for kc in range(16):
                                                P.pe(MM(psv(b, 0, 128, 0, [[1, 512]]), W1.v(0, 128, kc * 512 + cg * 128, [[1, 128]]),
                                                        h2T.v(0, 128, kc * TL + sub * 512, [[1, 512]]), start=(kc == 0), stop=(kc == 15)),
                                                     [W1.b, h2T.b], [PB[b]])
                                            P.act(ACT(RL.full(), psv(b, 0, 128, 0, [[1, 512]]), AF.Relu), [PB[b]], [RL.b])
                                            P.dve(TT(aT.v(0, 128, fc * TL + sub * 512, [[1, 512]]), RL.full(), RL.full(), ALU.mult), [RL.b], [aT.b])
                            if True:
                                for cs in range(8):
                                    W2 = w2[(half * 8 + cs) % 2]
                                    src_w = w_f2[l][half * 4096:(half + 1) * 4096, cs * 256:(cs + 1) * 256].rearrange("(kc p) c -> p kc c", p=128)
                                    P.dma("pool", W2.v(0, 128, 0, [[256, 32], [1, 256]]), src_w, W2.b, db("w"))
                                    for ti in range(TL // 128):
                                        k = c2
                                        c2 += 1
                                        XP, XO = xp[k % 2], xo[k % 2]
                                        r0 = ti * 128
                                        P.dma("sp", XP.full(), src[r0:r0 + 128, cs * 256:(cs + 1) * 256], XP.b, src_b)
                                        b = 4 + k % 4
                                        for fc in range(32):
                                            P.pe(MM(psv(b, 0, 128, 0, [[1, 256]]), aT.v(0, 128, fc * TL + ti * 128, [[1, 128]]),
                                                    W2.v(0, 128, fc * 256, [[1, 256]]), start=(fc == 0), stop=(fc == 31)), [aT.b, W2.b], [PB[b]])
                                        P.dve(TT(XO.full(), psv(b, 0, 128, 0, [[1, 256]]), XP.full(), ALU.add), [PB[b], XP.b], [XO.b])
                                        P.dma("sp", dst[r0:r0 + 128, cs * 256:(cs + 1) * 256], XO.full(), dst_b, XO.b)

            cur, cur_b = x_in, db("x")
            for l in range(n_layers):
                with Frame(P) as fh:
                    hT = fh.tile("hT", [128, 16, TL], BF16)
                    with Frame(P) as fn:
                        norm_T(fn, cur, cur_b, 0, TL // 128, l * NPP + 0, hT, 0)
                    phase_P(l, hT)
                    if stop_after == "P":
                        break
                    with Frame(P) as fd:
                        xi = phase_X1(0)
                        cnt_ = 0
                        for _ in gates_stream(l, hT, fd):
                            if cnt_ % XQ == 0 and xi is not None:
                                try:
                                    next(xi)
                                except StopIteration:
                                    xi = None
                            cnt_ += 1
                        if xi is not None:
                            for _ in xi:
                                pass
                        for _ in phase_X1(1):
                            pass
                    if stop_after == "X1":
                        break
                for _ in phase_G0(l):
                    pass
                for _ in phase_G(l):
                    pass
                if stop_after == "G":
                    break
                phase_S(l)
                with Frame(P):
                    phase_X2()
                if stop_after == "S":
                    break
                last = (l == n_layers - 1)
                phase_O(l, cur, cur_b, XS[0], db("XS0"))
                if stop_after == "O":
                    break
                outT, outB = (y_out, db("y")) if last else (XS[2], db("XS2"))
                phase_F(l, XS[0], db("XS0"), XS[1], db("XS1"), outT, outB)
                cur, cur_b = XS[2], db("XS2")
        P.emit()
    return nc


def _host_inputs(inputs):
    f = lambda k: np.ascontiguousarray(np.asarray(inputs[k], dtype=np.float32))
    cf, cb = _const_packs()
    pp = np.zeros((128, L * NPP), np.float32)
    nm, nl = f("norm_mix_g"), f("norm_mlp_g")
    cw = f("conv_w")
    cwp = np.zeros((L, 2, 128, 48), np.float32)
    for l in range(L):
        o = l * NPP
        pp[:, o:o + 16] = nm[l].reshape(16, 128).T
        pp[:, o + 16:o + 32] = nl[l].reshape(16, 128).T
        pp[:, o + 128] = f("gdn_norm_g")[l]
        pp[:, o + 129] = f("sba_q_g")[l]
        pp[:, o + 130] = f("sba_k_g")[l]
        c5 = cw[l].reshape(4, 3, 2, 4, 128)
        cwp[l] = c5.transpose(2, 4, 1, 3, 0).reshape(2, 128, 48)
    ab = np.concatenate([f("a_log"), f("dt_bias")], axis=1).reshape(1, L * 16)
    wsT = np.ascontiguousarray(f("w_spatial").transpose(0, 3, 1, 2).reshape(L, 128, 1024))
    shared = {
        "w_in": f("w_in"), "w_out_a": f("w_out_a"), "w_out_b": f("w_out_b"), "w_out_c": f("w_out_c"),
        "w_out": f("w_out"), "w_ff1": f("w_ff1"), "w_ff2": f("w_ff2"),
        "cf": cf, "cb": cb, "pp": pp, "cw": np.ascontiguousarray(cwp.reshape(L * 2 * 128, 48)), "ab": ab,
        "lng": f("gmlp_ln_g"), "wsT": wsT, "bsp": f("b_spatial").reshape(L, 1024),
    }
    return shared


def kernel(**inputs):
    x = np.ascontiguousarray(np.asarray(inputs["x"], dtype=np.float32))
    shared = _host_inputs(inputs)
    nc = build()
    in_maps = [dict(shared, x=np.ascontiguousarray(x[c // 2, (c % 2) * TL:(c % 2 + 1) * TL])) for c in range(NCORES)]
    res = run_bass_kernel_spmd(nc, in_maps, core_ids=list(range(NCORES)))
    out = np.empty((4, T, D), np.float32)
    for c in range(NCORES):
        out[c // 2, (c % 2) * TL:(c % 2 + 1) * TL] = res.results[c]["y"]
    return out
```

```python
import contextlib
import os
import numpy as np
import concourse.bass as bass
import concourse.mybir as mybir
from concourse.bass_utils import run_bass_kernel_spmd

F32 = mybir.dt.float32
BF16 = mybir.dt.bfloat16
AF = mybir.ActivationFunctionType
ALU = mybir.AluOpType
AX = mybir.AxisListType

T = 2048
TL = 1024
D = 2048
L = 4
DIN = 15376
DFF = 8192
NCORES = 8
EPS = 1e-6
NEG = -30000.0
NPP = 131


def _h(t):
    return t.tensor if hasattr(t, "tensor") else t


class Buf:
    __slots__ = ("name", "w", "r", "sem", "cnt")

    def __init__(self, name):
        self.name = name
        self.w = None
        self.r = {}
        self.sem = None
        self.cnt = 0


class Tl:
    def __init__(self, t, shape, buf):
        self.t = _h(t)
        self.shape = list(shape)
        self.F = int(np.prod(shape[1:]))
        self.b = buf

    def v(self, p0, pn, off, dims):
        return bass.AP(self.t, p0 * self.F + off, [[self.F, pn]] + [list(d) for d in dims])

    def full(self):
        return self.v(0, self.shape[0], 0, [[1, self.F]])


class Prog:
    ENGS = ("pe", "act", "dve", "pool", "sp")

    def __init__(self, nc, stack):
        self.nc = nc
        self.stack = stack
        self.sems = {}
        for e in self.ENGS:
            self.sems[e] = stack.enter_context(nc.semaphore("s_" + e))
        self.cnt = {e: 0 for e in self.ENGS}
        self.ops = {e: [] for e in self.ENGS}
        self.seen = {e: {} for e in self.ENGS}
        self.dirty = {}
        self.uid = 0
        self.sem_pool = []
        self.nsem = 0

    def buf(self, name="b"):
        self.uid += 1
        return Buf(f"{name}{self.uid}")

    def _filter(self, eng, deps, barrier=False):
        out = []
        seen = self.seen[eng]
        for k, v in deps.items():
            if (not barrier) and eng == "pe" and k == "pe":
                continue
            if seen.get(k, 0) >= v:
                continue
            seen[k] = v
            out.append((k, v))
        return out

    def op(self, eng, fn, reads=(), writes=()):
        deps = {}

        def add(d):
            if d is not None and deps.get(d[0], 0) < d[1]:
                deps[d[0]] = d[1]

        for b in reads:
            add(b.w)
        for b in writes:
            add(b.w)
            for kv in b.r.items():
                add(kv)
        waits = self._filter(eng, deps)
        self.cnt[eng] += 1
        c = self.cnt[eng]
        for b in reads:
            b.r[eng] = c
        for b in writes:
            b.w = (eng, c)
            b.r = {}
        self.ops[eng].append((waits, fn, eng, 1))

    def pe(self, fn, r=(), w=()):
        self.op("pe", fn, r, w)

    def act(self, fn, r=(), w=()):
        self.op("act", fn, r, w)

    def dve(self, fn, r=(), w=()):
        self.op("dve", fn, r, w)

    def pool(self, fn, r=(), w=()):
        self.op("pool", fn, r, w)

    def _getsem(self, b):
        if b.sem is None:
            if self.sem_pool:
                b.sem, b.cnt = self.sem_pool.pop()
            else:
                self.nsem += 1
                b.sem = f"d{self.nsem}"
                self.sems[b.sem] = self.stack.enter_context(self.nc.semaphore(b.sem))
                b.cnt = 0

    def dma(self, q, out_ap, in_ap, dst, src, **kw):
        self._getsem(dst)
        deps = {}

        def add(d):
            if d is not None and deps.get(d[0], 0) < d[1]:
                deps[d[0]] = d[1]

        add(src.w)
        if dst.w is not None and not (dst.w[0] == dst.sem and not dst.r):
            add(dst.w)
        for kv in dst.r.items():
            add(kv)
        waits = self._filter(q, deps)
        dst.cnt += 16
        src.r[dst.sem] = dst.cnt
        dst.w = (dst.sem, dst.cnt)
        dst.r = {}
        self.dirty[dst.sem] = dst.cnt
        self.ops[q].append((waits, (lambda e: e.dma_start(out=out_ap, in_=in_ap, **kw)), dst.sem, 16))

    def cc(self, in_ap, out_ap, dst, src, groups):
        self._getsem(dst)
        deps = {}

        def add(d):
            if d is not None and deps.get(d[0], 0) < d[1]:
                deps[d[0]] = d[1]

        add(src.w)
        add(dst.w)
        for kv in dst.r.items():
            add(kv)
        waits = self._filter("pool", deps)
        dst.cnt += 1
        src.r[dst.sem] = dst.cnt
        dst.w = (dst.sem, dst.cnt)
        dst.r = {}
        self.dirty[dst.sem] = dst.cnt
        self.ops["pool"].append((waits, (lambda e: e.collective_compute(
            "AllGather", ALU.bypass, replica_groups=groups, ins=[in_ap], outs=[out_ap])), dst.sem, 1))

    def barrier(self):
        targets = {e: self.cnt[e] for e in self.ENGS if self.cnt[e] > 0}
        targets.update(self.dirty)
        self.dirty = {}
        for e in self.ENGS:
            waits = self._filter(e, targets, barrier=True)
            if waits:
                self.ops[e].append((waits, None, None, 0))

    def emit(self):
        ops = self.ops
        self.ops = {e: [] for e in self.ENGS}
        sems = self.sems

        def mk(eng):
            def body(e):
                for waits, fn, sk, inc in ops[eng]:
                    for k, v in waits:
                        e.wait_ge(sems[k], v)
                    if fn is not None:
                        fn(e).then_inc(sems[sk], inc)
            return body

        with self.nc.Block() as blk:
            blk.tensor(mk("pe"))
            blk.scalar(mk("act"))
            blk.vector(mk("dve"))
            blk.gpsimd(mk("pool"))
            blk.sync(mk("sp"))


class Frame:
    def __init__(self, P):
        self.P = P
        self.st = contextlib.ExitStack()
        self.bufs = []

    def __enter__(self):
        return self

    def tile(self, name, shape, dt):
        P = self.P
        P.uid += 1
        t = self.st.enter_context(P.nc.sbuf_tensor(f"{name}_{P.uid}", list(shape), dt))
        b = P.buf(name)
        self.bufs.append(b)
        return Tl(t, shape, b)

    def __exit__(self, et, ev, tb):
        if et is None:
            self.P.barrier()
            for b in self.bufs:
                if b.sem is not None:
                    self.P.sem_pool.append((b.sem, b.cnt))
        self.st.close()
        return False


def MM(out, lhsT, rhs, start=True, stop=True):
    return lambda e: e.matmul(out, lhsT=lhsT, rhs=rhs, start=start, stop=stop)


def TR(out, in_, ident):
    return lambda e: e.transpose(out, in_, ident)


def ACT(out, in_, func, bias=None, scale=None):
    kw = {}
    if bias is not None:
        kw["bias"] = bias
    if scale is not None:
        kw["scale"] = scale
    return lambda e: e.activation(out=out, in_=in_, func=func, **kw)


def TT(out, in0, in1, op):
    return lambda e: e.tensor_tensor(out=out, in0=in0, in1=in1, op=op)


def TS(out, in0, s1, s2, op0, op1=None):
    if op1 is None:
        return lambda e: e.tensor_scalar(out=out, in0=in0, scalar1=s1, scalar2=None, op0=op0)
    return lambda e: e.tensor_scalar(out=out, in0=in0, scalar1=s1, scalar2=s2, op0=op0, op1=op1)


def STT(out, in0, scalar, in1, op0, op1):
    return lambda e: e.scalar_tensor_tensor(out=out, in0=in0, scalar=scalar, in1=in1, op0=op0, op1=op1)


def CP(out, in_):
    return lambda e: e.tensor_copy(out=out, in_=in_)


C_ID, C_ONE, C_TRIU, C_MBL, C_MBU, C_NSTR, C_F32END = 0, 128, 256, 384, 448, 512, 576
CB_ONE, CB_TRIS, CB_MASK, CB_END = 0, 128, 256, 256 + 2048


def _const_packs():
    cf = np.zeros((128, C_F32END), np.float32)
    cf[:, C_ID:C_ID + 128] = np.eye(128, dtype=np.float32)
    cf[:, C_ONE:C_ONE + 128] = 1.0
    j = np.arange(128)[:, None]
    c = np.arange(128)[None, :]
    cf[:, C_TRIU:C_TRIU + 128] = (j <= c).astype(np.float32)
    p = np.arange(64)[:, None]
    f = np.arange(64)[None, :]
    cf[0:64, C_MBL:C_MBL + 64] = np.where(f <= p, 0.0, NEG)
    cf[0:64, C_MBU:C_MBU + 64] = np.where(f >= p, 0.0, NEG)
    cf[0:64, C_NSTR:C_NSTR + 64] = np.where(f < p, -1.0, 0.0)
    cb = np.zeros((128, CB_END), np.float32)
    cb[:, CB_ONE:CB_ONE + 128] = 1.0
    cb[:, CB_TRIS:CB_TRIS + 128] = (j >= c).astype(np.float32)
    pp = np.arange(128)[:, None]
    ff = np.arange(512)[None, :]
    for i in range(4):
        cb[:, CB_MASK + i * 512:CB_MASK + (i + 1) * 512] = (ff > i * 128 + pp).astype(np.float32)
    return cf, cb


def build(n_layers=L, stop_after=None, debug=False, n_cores=NCORES, QA=2, QD=1, XQ=64):
    nc = bass.Bass("TRN2", target_bir_lowering=False)
    skind = "ExternalOutput" if debug else "Internal"
    groups = [[2 * i, 2 * i + 1] for i in range(n_cores // 2)]
    pid = nc.partition_id()
    rank = pid % 2
    other = 1 - rank

    def din(name, shape, dt=F32):
        return nc.dram_tensor(name, list(shape), dt, kind="ExternalInput").ap()

    def dscr(name, shape, dt, internal=False):
        return nc.dram_tensor(name, list(shape), dt, kind=("Internal" if internal else skind)).ap()

    x_in = din("x", [TL, D])
    w_in = din("w_in", [L, D, DIN])
    w_oa = din("w_out_a", [L, 1024, D])
    w_ob = din("w_out_b", [L, 1024, D])
    w_oc = din("w_out_c", [L, 1024, D])
    w_o = din("w_out", [L, D, D])
    w_f1 = din("w_ff1", [L, D, DFF])
    w_f2 = din("w_ff2", [L, DFF, D])
    cf_in = din("cf", [128, C_F32END])
    cb_in = din("cb", [128, CB_END])
    pp_in = din("pp", [128, L * NPP])
    cw_in = din("cw", [L * 2 * 128, 48])
    ab_in = din("ab", [1, L * 16])
    lng_in = din("lng", [L, 1024])
    wsT_in = din("wsT", [L, 128, 1024])
    bsp_in = din("bsp", [L, 1024])
    y_out = nc.dram_tensor("y", [TL, D], F32, kind="ExternalOutput").ap()

    ZL = [dscr(f"ZL{k}", [1024, TL], F32) for k in range(4)]
    SENDZ = [dscr(f"SENDZ{k}", [512, TL], F32, True) for k in range(4)]
    GZ = [dscr(f"GZ{k}", [1024, TL], F32, True) for k in range(4)]
    FZ = [dscr(f"FZ{k}", [1024, TL], F32) for k in range(4)]
    SQL = [dscr(f"SQL{k}", [1024, TL], BF16) for k in range(2)]
    SENDQ = [dscr(f"SENDQ{k}", [512, TL], BF16, True) for k in range(2)]
    GQ = [dscr(f"GQ{k}", [1024, TL], BF16, True) for k in range(2)]
    FSQ = [dscr(f"FSQ{k}", [1024, TL], BF16) for k in range(2)]
    SVL = dscr("SVL", [2 * TL, 512], BF16)
    SENDV = dscr("SENDV", [TL, 512], BF16, True)
    GV = dscr("GV", [2 * TL, 512], BF16, True)
    FSV = dscr("FSV", [2 * TL, 512], BF16)
    GBL = dscr("GBL", [2 * TL, 8], F32)
    SENDG = dscr("SENDG", [TL, 8], F32, True)
    GGB = dscr("GGB", [2 * TL, 8], F32, True)
    FGB = dscr("FGB", [2 * TL, 8], F32)
    QT = dscr("QT", [4, 128, T], F32)
    KT = dscr("KT", [4, 128, T], F32)
    VT = dscr("VT", [4, 128, T], F32)
    OAm = dscr("OAm", [1024, TL], BF16)
    OCm = dscr("OCm", [1024, TL], BF16)
    SENDO = [dscr(f"SENDO{k}", [512, TL], BF16, True) for k in range(2)]
    GO = [dscr(f"GO{k}", [1024, TL], BF16, True) for k in range(2)]
    BRA = dscr("BRA", [1024, TL], BF16)
    BRC = dscr("BRC", [1024, TL], BF16)
    UT = dscr("UT", [8, 128, TL], F32)
    UBT = dscr("UBT", [8, 128, TL], BF16)
    GATES = dscr("GATES", [48, 128, TL], BF16)
    XS = [dscr(f"XS{i}", [TL, D], F32) for i in range(3)]

    with contextlib.ExitStack() as stack:
        P = Prog(nc, stack)
        PS = [_h(stack.enter_context(nc.psum_tensor(f"ps{i}", [128, 1024], F32))) for i in range(4)]
        PB = [P.buf(f"psb{i}") for i in range(8)]

        def psv(b, p0, pn, off, dims):
            return bass.AP(PS[b // 2], p0 * 1024 + (b % 2) * 512 + off, [[1024, pn]] + [list(d) for d in dims])

        DB = {}

        def db(name):
            if name not in DB:
                DB[name] = P.buf(name)
            return DB[name]

        with Frame(P) as G:
            CF = G.tile("CF", [128, C_F32END], F32)
            CBt = G.tile("CB", [128, CB_END], BF16)
            PPt = G.tile("PP", [128, L * NPP], F32)
            ABt = G.tile("AB", [128, L * 16], F32)
            NEA = G.tile("NEA", [128, L * 8], F32)
            SQS = G.tile("SQS", [128, L], F32)
            P.dma("sp", CF.full(), cf_in, CF.b, db("cst"))
            P.dma("pool", CBt.full(), cb_in, CBt.b, db("cst"))
            P.dma("sp", PPt.full(), pp_in, PPt.b, db("cst"))
            P.dma("sp", ABt.full(), bass.AP(ab_in.tensor, 0, [[0, 128], [1, L * 16]]), ABt.b, db("cst"))
            for l in range(L):
                P.act(ACT(NEA.v(0, 128, l * 8, [[1, 8]]), ABt.v(0, 128, l * 16, [[1, 8]]), AF.Exp), [ABt.b], [NEA.b])
            P.dve(TS(NEA.full(), NEA.full(), -1.0, None, ALU.mult), [NEA.b], [NEA.b])
            for l in range(L):
                P.dve(TS(SQS.v(0, 128, l, [[1, 1]]), PPt.v(0, 128, l * NPP + 129, [[1, 1]]), float(128 ** -0.5), None, ALU.mult),
                      [PPt.b], [SQS.b])

            ident = CF.v(0, 128, C_ID, [[1, 128]])
            ones_f = CF.v(0, 128, C_ONE, [[1, 128]])
            ones_b = CBt.v(0, 128, CB_ONE, [[1, 128]])
            EPSc = float(EPS)

            def norm_T(fr, xsrc, xbuf, tok0, ntiles, gcol, hT, tcol0):
                xt = [fr.tile("xt", [128, D], F32) for _ in range(2)]
                sq = fr.tile("sq", [128, D], F32)
                xs = [fr.tile("xs", [128, D], F32) for _ in range(2)]
                st = [fr.tile("st", [128, 4], F32) for _ in range(2)]
                NTK = hT.shape[2]
                for i in range(ntiles):
                    a = i % 2
                    X, S_, Xs = xt[a], st[a], xs[a]
                    r0 = tok0 + i * 128
                    P.dma("sp", X.full(), xsrc[r0:r0 + 128, :], X.b, xbuf)
                    P.act(ACT(sq.full(), X.full(), AF.Square), [X.b], [sq.b])
                    P.dve(lambda e, o=S_.v(0, 128, 0, [[1, 1]]), i_=sq.full(): e.reduce_sum(out=o, in_=i_, axis=AX.X), [sq.b], [S_.b])
                    P.act(ACT(S_.v(0, 128, 1, [[1, 1]]), S_.v(0, 128, 0, [[1, 1]]), AF.Ln, bias=EPSc, scale=1.0 / D), [S_.b], [S_.b])
                    P.act(ACT(S_.v(0, 128, 2, [[1, 1]]), S_.v(0, 128, 1, [[1, 1]]), AF.Exp, scale=-0.5), [S_.b], [S_.b])
                    P.act(ACT(Xs.full(), X.full(), AF.Copy, scale=S_.v(0, 128, 2, [[1, 1]])), [X.b, S_.b], [Xs.b])
                    bb = 4 * a
                    for kc in range(16):
                        b = bb + kc // 4
                        P.pe(TR(psv(b, 0, 128, (kc % 4) * 128, [[1, 128]]), Xs.v(0, 128, kc * 128, [[1, 128]]), ident),
                             [Xs.b, CF.b], [PB[b]])
                    for q in range(4):
                        b = bb + q
                        P.dve(TT(hT.v(0, 128, (4 * q) * NTK + tcol0 + i * 128, [[NTK, 4], [1, 128]]),
                                 psv(b, 0, 128, 0, [[128, 4], [1, 128]]),
                                 PPt.v(0, 128, gcol + 4 * q, [[1, 4], [0, 128]]), ALU.mult),
                              [PB[b], PPt.b], [hT.b])

            def load_slab(fr_tile, wap, r0, nk, c0, ncols):
                src = wap[r0:r0 + nk * 128, c0:c0 + ncols].rearrange("(kc p) c -> p kc c", p=128)
                P.dma("pool", fr_tile.v(0, 128, 0, [[ncols, nk], [1, ncols]]), src, fr_tile.b, db("w"))

            def phase_P(l, hT):
                pb = l * NPP
                wl = w_in[l]
                NTG = TL // 512
                with Frame(P) as fr:
                    wsl = [fr.tile("wsl", [128, 16, 512], BF16) for _ in range(2)]
                    zrow = [fr.tile("zrow", [128, TL], F32) for _ in range(2)]
                    racc = fr.tile("racc", [128, TL], F32)
                    rC = fr.tile("rC", [128, TL], F32)
                    rout = [fr.tile("rout", [128, TL], F32) for _ in range(2)]
                    routb = [fr.tile("routb", [128, TL], BF16) for _ in range(2)]
                    state = {"slab": 0, "bank": 0}

                    def fm_rows(c0, nrows, consume):
                        for s0 in range(0, nrows, 4):
                            W = wsl[state["slab"] % 2]
                            state["slab"] += 1
                            load_slab(W, wl, 0, 16, c0 + s0 * 128, 512)
                            for cg in range(4):
                                ri = s0 + cg
                                for tg in range(NTG):
                                    b = state["bank"] % 4
                                    state["bank"] += 1
                                    for kc in range(16):
                                        P.pe(MM(psv(b, 0, 128, 0, [[1, 512]]),
                                                W.v(0, 128, kc * 512 + cg * 128, [[1, 128]]),
                                                hT.v(0, 128, kc * TL + tg * 512, [[1, 512]]),
                                                start=(kc == 0), stop=(kc == 15)),
                                             [W.b, hT.b], [PB[b]])
                                    consume(ri, tg, b)
                                consume(ri, None, None)

                    def norm_sums(src, scale, bias):
                        for tg in range(NTG):
                            b = 4 + tg
                            P.pe(MM(psv(b, 0, 128, 0, [[1, 512]]), ones_f, src.v(0, 128, tg * 512, [[1, 512]])),
                                 [src.b, CF.b], [PB[b]])
                            P.act(ACT(rC.v(0, 128, tg * 512, [[1, 512]]), psv(b, 0, 128, 0, [[1, 512]]), AF.Ln,
                                      bias=bias, scale=scale), [PB[b]], [rC.b])

                    def c_gdn(ri, tg, b):
                        kind, h = ri // 8, ri % 8
                        R = rout[ri % 2]
                        if tg is not None:
                            if (ri + tg) % 2 == 0:
                                P.act(ACT(R.v(0, 128, tg * 512, [[1, 512]]), psv(b, 0, 128, 0, [[1, 512]]), AF.Copy), [PB[b]], [R.b])
                            else:
                                P.dve(CP(R.v(0, 128, tg * 512, [[1, 512]]), psv(b, 0, 128, 0, [[1, 512]])), [PB[b]], [R.b])
                            return
                        P.dma("sp", ZL[kind][h * 128:(h + 1) * 128, :], R.full(), db(f"ZL{kind}"), R.b)

                    fm_rows(0, 24, c_gdn)

                    def c_gate(ri, tg, b):
                        R = rout[ri % 2]
                        if tg is not None:
                            P.act(ACT(R.v(0, 128, tg * 512, [[1, 512]]), psv(b, 0, 128, 0, [[1, 512]]), AF.Silu), [PB[b]], [R.b])
                            return
                        P.dma("sp", ZL[3][ri * 128:(ri + 1) * 128, :], R.full(), db("ZL3"), R.b)

                    fm_rows(3088, 8, c_gate)

                    def c_u(ri, tg, b):
                        R = rout[ri % 2]
                        if tg is not None:
                            P.act(ACT(R.v(0, 128, tg * 512, [[1, 512]]), psv(b, 0, 128, 0, [[1, 512]]), AF.Gelu), [PB[b]], [R.b])
                            return
                        P.dma("sp", UT[ri], R.full(), db("UT"), R.b)

                    fm_rows(4112, 8, c_u)

                    def c_sqk(ri, tg, b):
                        kind, h = ri // 8, ri % 8
                        Z = zrow[ri % 2]
                        RB = routb[ri % 2]
                        if tg is not None:
                            P.act(ACT(Z.v(0, 128, tg * 512, [[1, 512]]), psv(b, 0, 128, 0, [[1, 512]]), AF.Copy), [PB[b]], [Z.b])
                            return
                        P.act(ACT(racc.full(), Z.full(), AF.Square), [Z.b], [racc.b])
                        norm_sums(racc, 1.0 / 128.0, EPSc)
                        P.act(ACT(rC.full(), rC.full(), AF.Exp, scale=-0.5), [rC.b], [rC.b])
                        gcol = SQS.v(0, 128, l, [[1, 1]]) if kind == 0 else PPt.v(0, 128, pb + 130, [[1, 1]])
                        P.dve(STT(RB.full(), Z.full(), gcol, rC.full(), ALU.mult, ALU.mult), [Z.b, rC.b, SQS.b, PPt.b], [RB.b])
                        P.dma("sp", SQL[kind][h * 128:(h + 1) * 128, :], RB.full(), db(f"SQL{kind}"), RB.b)

                    fm_rows(6160, 16, c_sqk)


                    NCH = TL // 64
                    wab = fr.tile("wab", [128, 16, 16], BF16)
                    load_slab(wab, wl, 0, 16, 3072, 16)
                    tmp8 = fr.tile("tmp8", [64, 16], F32)
                    gbt = fr.tile("gbt", [64, NCH * 16], F32)
                    for n in range(NCH):
                        b = n % 4
                        for kc in range(16):
                            P.pe(MM(psv(b, 0, 64, 0, [[1, 16]]), hT.v(0, 128, kc * TL + n * 64, [[1, 64]]),
                                    wab.v(0, 128, kc * 16, [[1, 16]]), start=(kc == 0), stop=(kc == 15)),
                                 [hT.b, wab.b], [PB[b]])
                        P.dve(TT(tmp8.v(0, 64, 0, [[1, 8]]), psv(b, 0, 64, 0, [[1, 8]]), ABt.v(0, 64, l * 16 + 8, [[1, 8]]), ALU.add),
                              [PB[b], ABt.b], [tmp8.b])
                        P.act(ACT(tmp8.v(0, 64, 0, [[1, 8]]), tmp8.v(0, 64, 0, [[1, 8]]), AF.Exp), [tmp8.b], [tmp8.b])
                        P.act(ACT(tmp8.v(0, 64, 0, [[1, 8]]), tmp8.v(0, 64, 0, [[1, 8]]), AF.Ln, bias=1.0), [tmp8.b], [tmp8.b])
                        P.dve(TT(gbt.v(0, 64, n * 16, [[8, 2], [1, 4]]), tmp8.v(0, 64, 0, [[4, 2], [1, 4]]),
                                 NEA.v(0, 64, l * 8, [[4, 2], [1, 4]]), ALU.mult), [tmp8.b, NEA.b], [gbt.b])
                        P.act(ACT(gbt.v(0, 64, n * 16 + 4, [[8, 2], [1, 4]]), psv(b, 0, 64, 8, [[4, 2], [1, 4]]), AF.Sigmoid), [PB[b]], [gbt.b])
                    for hg in range(2):
                        P.dma("sp", GBL[hg * TL:(hg + 1) * TL, :].rearrange("(n c) k -> c n k", c=64),
                              gbt.v(0, 64, hg * 8, [[16, NCH], [1, 8]]), db("GBL"), gbt.b)

                    wA = fr.tile("wA", [128, 16, 512], BF16)
                    wB = fr.tile("wB", [128, 16, 512], BF16)
                    wC = fr.tile("wC", [128, 16, 512], BF16)
                    NTI = TL // 128

                    def tm_tile(tile_i, slabs, banks):
                        for si, W in enumerate(slabs):
                            b = banks[si]
                            for kc in range(16):
                                P.pe(MM(psv(b, 0, 128, 0, [[1, 512]]), hT.v(0, 128, kc * TL + tile_i * 128, [[1, 128]]),
                                        W.v(0, 128, kc * 512, [[1, 512]]), start=(kc == 0), stop=(kc == 15)),
                                     [hT.b, W.b], [PB[b]])

                    load_slab(wA, wl, 0, 16, 8208, 512)
                    load_slab(wB, wl, 0, 16, 8208 + 512, 512)
                    vt = [fr.tile("vt", [128, 1024], BF16) for _ in range(2)]
                    for ti in range(NTI):
                        bk = [0, 1] if ti % 2 == 0 else [2, 3]
                        tm_tile(ti, [wA, wB], bk)
                        V_ = vt[ti % 2]
                        for si in range(2):
                            P.act(ACT(V_.v(0, 128, si * 512, [[1, 512]]), psv(bk[si], 0, 128, 0, [[1, 512]]), AF.Copy),
                                  [PB[bk[si]]], [V_.b])
                        for hg in range(2):
                            P.dma("sp", SVL[hg * TL + ti * 128:hg * TL + (ti + 1) * 128, :], V_.v(0, 128, hg * 512, [[1, 512]]),
                                  db("SVL"), V_.b)

                    load_slab(wA, wl, 0, 16, 5136, 512)
                    load_slab(wC, wl, 0, 16, 5136 + 512, 512)
                    lng = fr.tile("lng", [128, 1024], F32)
                    P.dma("sp", lng.full(), bass.AP(lng_in.tensor, l * 1024, [[0, 128], [1, 1024]]), lng.b, db("cst"))
                    wsf = fr.tile("wsf", [128, 1024], F32)
                    P.dma("sp", wsf.full(), wsT_in[l], wsf.b, db("cst"))
                    wsb = fr.tile("wsb", [128, 1024], BF16)
                    P.dve(CP(wsb.full(), wsf.full()), [wsf.b], [wsb.b])
                    P.dve(lambda e, a=wsb.v(64, 64, 0, [[128, 8], [1, 64]]): e.memset(a, 0.0), [wsb.b], [wsb.b])
                    bspb = fr.tile("bspb", [1, 1024], BF16)
                    P.dma("pool", bspb.full(), bass.AP(bsp_in.tensor, l * 1024, [[0, 1], [1, 1024]]), bspb.b, db("cst"))
                    vg = fr.tile("vg", [128, 1024], F32)
                    vb = fr.tile("vb", [128, 1024], BF16)
                    bst = fr.tile("bst", [128, 16], F32)
                    ut = [fr.tile("ut", [128, 8, 128], F32) for _ in range(2)]
                    ubt = [fr.tile("ubt", [128, 8, 128], BF16) for _ in range(2)]
                    for ti in range(NTI):
                        U_ = ut[ti % 2]
                        UB_ = ubt[ti % 2]
                        P.dma("sp", U_.v(0, 128, 0, [[128, 8], [1, 128]]),
                              UT[:, :, ti * 128:(ti + 1) * 128].rearrange("g c t -> c g t"), U_.b, db("UT"))
                        bk = [0, 1]
                        tm_tile(ti, [wA, wC], bk)
                        for si in range(2):
                            P.act(ACT(vg.v(0, 128, si * 512, [[1, 512]]), psv(bk[si], 0, 128, 0, [[1, 512]]), AF.Gelu),
                                  [PB[bk[si]]], [vg.b])
                        for si in range(2):
                            P.dve(lambda e, o=bst.v(0, 128, si * 6, [[1, 6]]), i_=vg.v(0, 128, si * 512, [[1, 512]]): e.bn_stats(o, i_),
                                  [vg.b], [bst.b])
                        P.dve(lambda e, o=bst.v(0, 128, 12, [[1, 2]]), i_=bst.v(0, 128, 0, [[1, 12]]): e.bn_aggr(o, i_), [bst.b], [bst.b])
                        P.act(ACT(bst.v(0, 128, 14, [[1, 1]]), bst.v(0, 128, 13, [[1, 1]]), AF.Ln, bias=EPSc), [bst.b], [bst.b])
                        P.act(ACT(bst.v(0, 128, 15, [[1, 1]]), bst.v(0, 128, 14, [[1, 1]]), AF.Exp, scale=-0.5), [bst.b], [bst.b])
                        P.dve(TS(vg.full(), vg.full(), bst.v(0, 128, 12, [[1, 1]]), bst.v(0, 128, 15, [[1, 1]]), ALU.subtract, ALU.mult),
                              [vg.b, bst.b], [vg.b])
                        P.dve(TT(vb.full(), vg.full(), lng.full(), ALU.mult), [vg.b, lng.b], [vb.b])
                        for g in range(8):
                            b = 2 + g // 4
                            o = psv(b, 0, 128, (g % 4) * 128, [[1, 128]])
                            P.pe(MM(o, vb.v(0, 128, g * 128, [[1, 128]]), wsb.v(0, 128, g * 128, [[1, 128]]), start=True, stop=False),
                                 [vb.b, wsb.b], [PB[b]])
                            P.pe(MM(o, CBt.v(0, 1, CB_ONE, [[1, 128]]), bspb.v(0, 1, g * 128, [[1, 128]]), start=False, stop=True),
                                 [CBt.b, bspb.b], [PB[b]])
                        for q in range(2):
                            P.dve(TT(UB_.v(0, 128, q * 512, [[1, 512]]), U_.v(0, 128, q * 512, [[1, 512]]),
                                     psv(2 + q, 0, 128, 0, [[1, 512]]), ALU.mult), [U_.b, PB[2 + q]], [UB_.b])
                        P.dma("sp", UBT[:, :, ti * 128:(ti + 1) * 128].rearrange("g c t -> c g t"),
                              UB_.v(0, 128, 0, [[128, 8], [1, 128]]), db("UBT"), UB_.b)

            def gates_stream(l, hT, fr):
                wl = w_in[l]
                wh = [fr.tile("wh", [128, 16, 256], BF16) for _ in range(2)]
                rb = [fr.tile("rbg", [128, TL], BF16) for _ in range(2)]
                yield
                k = 0
                for hs in range(24):
                    W = wh[hs % 2]
                    load_slab(W, wl, 0, 16, 9232 + hs * 256, 256)
                    for cg in range(2):
                        ri = hs * 2 + cg
                        RB = rb[ri % 2]
                        for tg in range(TL // 512):
                            b = 4 + k % 4
                            k += 1
                            for kc in range(16):
                                P.pe(MM(psv(b, 0, 128, 0, [[1, 512]]), W.v(0, 128, kc * 256 + cg * 128, [[1, 128]]),
                                        hT.v(0, 128, kc * TL + tg * 512, [[1, 512]]), start=(kc == 0), stop=(kc == 15)),
                                     [W.b, hT.b], [PB[b]])
                                if kc % 2 == 1:
                                    yield
                            P.act(ACT(RB.v(0, 128, tg * 512, [[1, 512]]), psv(b, 0, 128, 0, [[1, 512]]), AF.Sigmoid), [PB[b]], [RB.b])
                        P.dma("sp", GATES[ri], RB.full(), db("GATES"), RB.b)

            def exchange(local, send, gath, full, name, n, part):
                if part in (0, 2):
                    P.dma("sp", send, local[bass.ts(other, n)], db("S" + name), db(name))
                    P.cc(send, gath, db("G" + name), db("S" + name), groups)
                    P.dma("sp", full[bass.ts(rank, n)], local[bass.ts(rank, n)], db("F" + name), db(name))
                if part in (1, 2):
                    P.dma("sp", full[bass.ts(other, n)], gath[bass.ts(other, n)], db("F" + name), db("G" + name))

            def phase_X1(part):
                for k in range(4):
                    exchange(ZL[k], SENDZ[k], GZ[k], FZ[k], f"ZL{k}", 512, part)
                    yield
                for k in range(2):
                    exchange(SQL[k], SENDQ[k], GQ[k], FSQ[k], f"SQL{k}", 512, part)
                    yield
                exchange(SVL, SENDV, GV, FSV, "SVL", TL, part)
                yield
                exchange(GBL, SENDG, GGB, FGB, "GBL", TL, part)
                yield

            def phase_X2():
                exchange(OAm, SENDO[0], GO[0], BRA, "OAm", 512, 2)
                exchange(OCm, SENDO[1], GO[1], BRC, "OCm", 512, 2)

            def phase_G0(l):
                with Frame(P) as fr:
                    cwm = fr.tile("cwm", [128, 48], F32)
                    P.dma("sp", cwm.full(), cw_in[bass.ts(rank + 2 * l, 128)], cwm.b, db("cst"))
                    DG = fr.tile("DG", [128, 48, 128], BF16)
                    for c in range(48):
                        P.dve(TS(DG.v(0, 128, c * 128, [[1, 128]]), ident, cwm.v(0, 128, c, [[1, 1]]), None, ALU.mult), [CF.b, cwm.b], [DG.b])
                    zb = [fr.tile("zb", [128, T + 4], BF16) for _ in range(2)]
                    rB = fr.tile("rB", [128, T], F32)
                    rsq = fr.tile("rsq", [128, T], BF16)
                    rC = fr.tile("rC", [128, T], F32)
                    rout = [fr.tile("rout", [128, T], F32) for _ in range(2)]
                    for z in zb:
                        P.dve(lambda e, a=z.v(0, 128, 0, [[1, 4]]): e.memset(a, 0.0), [], [z.b])
                    i = 0
                    for kind in range(3):
                        for hh in range(4):
                            Z, R = zb[i % 2], rout[i % 2]
                            i += 1
                            for half in range(2):
                                P.dma("pool", Z.v(0, 128, 4 + half * TL, [[1, TL]]),
                                      FZ[kind][half * 512 + hh * 128:half * 512 + (hh + 1) * 128, :], Z.b, db(f"FZL{kind}"))
                            dst = R if kind == 2 else rB
                            for tg in range(4):
                                b_ = tg
                                for j in range(4):
                                    P.pe(MM(psv(b_, 0, 128, 0, [[1, 512]]), DG.v(0, 128, ((kind * 4 + hh) * 4 + j) * 128, [[1, 128]]),
                                            Z.v(0, 128, 1 + j + tg * 512, [[1, 512]]), start=(j == 0), stop=(j == 3)), [DG.b, Z.b], [PB[b_]])
                                P.act(ACT(dst.v(0, 128, tg * 512, [[1, 512]]), psv(b_, 0, 128, 0, [[1, 512]]), AF.Silu), [PB[b_]], [dst.b])
                            if kind == 2:
                                P.dma("sp", VT[hh], R.full(), db("VT"), R.b)
                                yield
                                continue
                            P.act(ACT(rsq.full(), rB.full(), AF.Square), [rB.b], [rsq.b])
                            for tg in range(4):
                                b_ = 4 + tg
                                P.pe(MM(psv(b_, 0, 128, 0, [[1, 512]]), ones_b, rsq.v(0, 128, tg * 512, [[1, 512]])), [rsq.b, CBt.b], [PB[b_]])
                                P.act(ACT(rC.v(0, 128, tg * 512, [[1, 512]]), psv(b_, 0, 128, 0, [[1, 512]]), AF.Ln, bias=EPSc), [PB[b_]], [rC.b])
                            P.act(ACT(rC.full(), rC.full(), AF.Exp, scale=-0.5), [rC.b], [rC.b])
                            sc = float(128 ** -0.5) if kind == 0 else 1.0
                            P.dve(STT(R.full(), rB.full(), sc, rC.full(), ALU.mult, ALU.mult), [rB.b, rC.b], [R.b])
                            P.dma("sp", (QT if kind == 0 else KT)[hh], R.full(), db("QT" if kind == 0 else "KT"), R.b)
                            yield

            def phase_G(l):
                pb = l * NPP
                gng = PPt.v(0, 128, pb + 128, [[1, 1]])
                with Frame(P) as fr:
                    GBs = fr.tile("GBs", [64, 32 * 8], F32)
                    P.dma("sp", GBs.v(0, 64, 0, [[8, 32], [1, 8]]), FGB.rearrange("(n c) k -> c n k", c=64), GBs.b, db("FGBL"))
                    qS = fr.tile("qS", [128, 4, 512], F32)
                    kS = fr.tile("kS", [128, 4, 512], F32)
                    vS = fr.tile("vS", [128, 4, 512], F32)
                    gS = fr.tile("gS", [128, 4, 512], F32)
                    oaS = [fr.tile("oaS", [128, 4, 512], BF16) for _ in range(2)]
                    Sst = [fr.tile("S", [128, 4, 128], F32) for _ in range(2)]
                    gbc = fr.tile("gbc", [64, 16], F32)
                    gl = fr.tile("gl", [64, 8, 128], F32)
                    GbS = fr.tile("GbS", [128, 8, 64], F32)
                    sm = fr.tile("sm", [64, 48], F32)
                    t1 = fr.tile("t1", [64, 8, 64], F32)
                    t2 = fr.tile("t2", [64, 8, 64], F32)
                    dec = fr.tile("dec", [64, 8, 64], F32)
                    decT = fr.tile("decT", [64, 8, 64], F32)
                    rhsW = fr.tile("rhsW", [64, 8, 128], BF16)
                    rhsV = fr.tile("rhsV", [64, 8, 128], BF16)
                    kSb = fr.tile("kSb", [128, 4, 512], BF16)
                    qSb = fr.tile("qSb", [128, 4, 512], BF16)
                    Mb = fr.tile("Mb", [64, 8, 64], F32)
                    A0 = fr.tile("A0", [64, 8, 64], F32)
                    Rt = [fr.tile("R", [64, 8, 64], F32) for _ in range(2)]
                    PX = [fr.tile("PX", [64, 8, 128], F32) for _ in range(2)]
                    XSt = fr.tile("XSt", [64, 8, 64], BF16)
                    eGb2 = [fr.tile("eGb", [128, 8, 64], F32) for _ in range(2)]
                    kdec2 = [fr.tile("kdec", [64, 8, 128], BF16) for _ in range(2)]
                    uS2 = [fr.tile("uS", [64, 8, 128], F32) for _ in range(2)]
                    wTS2 = [fr.tile("wTS", [128, 8, 64], F32) for _ in range(2)]
                    intraT2 = [fr.tile("intraT", [64, 8, 64], BF16) for _ in range(2)]
                    qdT2 = [fr.tile("qdT", [128, 8, 64], F32) for _ in range(2)]
                    vnew = [fr.tile("vnew", [64, 4, 128], BF16) for _ in range(2)]
                    oS = [fr.tile("oS", [128, 4, 64], F32) for _ in range(2)]
                    o2 = [fr.tile("o2", [128, 4, 64], F32) for _ in range(2)]
                    rs = [fr.tile("rs", [128, 4, 64], F32) for _ in range(2)]
                    P.dve(lambda e, a=Sst[0].full(): e.memset(a, 0.0), [], [Sst[0].b])
                    id64 = CF.v(0, 64, C_ID, [[1, 64]])
                    triU64 = CF.v(0, 64, C_TRIU, [[1, 64]])

                    def pre(pi):
                        n0 = 2 * pi
                        sg, co = n0 // 8, (n0 % 8) * 64
                        eGb, kdec, uS, wTS, intraT, qdT = (eGb2[pi % 2], kdec2[pi % 2], uS2[pi % 2], wTS2[pi % 2],
                                                           intraT2[pi % 2], qdT2[pi % 2])
                        if n0 % 8 == 0:
                            for (Tt, Dr, nm) in ((qS, QT, "QT"), (kS, KT, "KT"), (vS, VT, "VT")):
                                P.dma("sp", Tt.v(0, 128, 0, [[512, 4], [1, 512]]),
                                      Dr[:, :, sg * 512:(sg + 1) * 512].rearrange("h d t -> d h t"), Tt.b, db(nm))
                            P.pool(CP(kSb.full(), kS.full()), [kS.b], [kSb.b])
                            P.pool(CP(qSb.full(), qS.full()), [qS.b], [qSb.b])

                        def uview(Tt, u, ncol=64):
                            cb, hh = divmod(u, 4)
                            return Tt.v(0, 128, hh * 512 + co + cb * 64, [[1, ncol]])

                        P.dve(CP(gbc.v(0, 64, 0, [[4, 2], [1, 4]]), GBs.v(0, 64, n0 * 8, [[8, 2], [1, 4]])), [GBs.b], [gbc.b])
                        P.dve(CP(gbc.v(0, 64, 8, [[4, 2], [1, 4]]), GBs.v(0, 64, n0 * 8 + 4, [[8, 2], [1, 4]])), [GBs.b], [gbc.b])
                        graw = gbc.v(0, 64, 0, [[1, 8]])
                        beta_bs = lambda w: gbc.v(0, 64, 8, [[1, 8], [0, w]])
                        P.dve(CP(gl.v(0, 64, 0, [[128, 8], [1, 128]]), gbc.v(0, 64, 0, [[1, 8], [0, 128]])), [gbc.b], [gl.b])
                        for u in range(8):
                            P.pe(MM(psv(2, 0, 128, u * 64, [[1, 64]]), gl.v(0, 64, u * 128, [[1, 128]]), triU64), [gl.b, CF.b], [PB[2]])
                        P.pe(MM(psv(0, 0, 64, 0, [[1, 8]]), triU64, graw), [gbc.b, CF.b], [PB[0]])
                        yield
                        P.act(ACT(GbS.full(), psv(2, 0, 128, 0, [[1, 512]]), AF.Copy), [PB[2]], [GbS.b])
                        P.act(ACT(eGb.full(), psv(2, 0, 128, 0, [[1, 512]]), AF.Exp), [PB[2]], [eGb.b])
                        P.dve(CP(sm.v(0, 64, 0, [[1, 8]]), psv(0, 0, 64, 0, [[1, 8]])), [PB[0]], [sm.b])
                        Gc_bs = sm.v(0, 64, 0, [[1, 8], [0, 64]])
                        P.dve(STT(t1.v(0, 64, 0, [[64, 8], [1, 64]]), GbS.v(0, 64, 0, [[64, 8], [1, 64]]), -1.0, Gc_bs, ALU.mult, ALU.add),
                              [GbS.b, sm.b], [t1.b])
                        P.pool(TT(t1.v(0, 64, 0, [[64, 8], [1, 64]]), t1.v(0, 64, 0, [[64, 8], [1, 64]]),
                                  CF.v(0, 64, C_MBL, [[0, 8], [1, 64]]), ALU.add), [t1.b, CF.b], [t1.b])
                        P.act(ACT(dec.full(), t1.full(), AF.Exp), [t1.b], [dec.b])
                        yield
                        P.dve(TT(t2.v(0, 64, 0, [[64, 8], [1, 64]]), GbS.v(0, 64, 0, [[64, 8], [1, 64]]), Gc_bs, ALU.subtract),
                              [GbS.b, sm.b], [t2.b])
                        P.pool(TT(t2.v(0, 64, 0, [[64, 8], [1, 64]]), t2.v(0, 64, 0, [[64, 8], [1, 64]]),
                                  CF.v(0, 64, C_MBU, [[0, 8], [1, 64]]), ALU.add), [t2.b, CF.b], [t2.b])
                        P.act(ACT(decT.full(), t2.full(), AF.Exp), [t2.b], [decT.b])
                        P.dve(TT(sm.v(0, 64, 8, [[1, 8]]), GbS.v(0, 64, 63, [[64, 8]]), sm.v(0, 64, 0, [[1, 8]]), ALU.subtract),
                              [GbS.b, sm.b], [sm.b])
                        P.act(ACT(sm.v(0, 64, 8, [[1, 8]]), sm.v(0, 64, 8, [[1, 8]]), AF.Exp), [sm.b], [sm.b])
                        P.act(ACT(sm.v(0, 64, 16, [[1, 8]]), sm.v(0, 64, 0, [[1, 8]]), AF.Exp), [sm.b], [sm.b])
                        P.dve(TT(sm.v(0, 64, 24, [[1, 8]]), sm.v(0, 64, 16, [[1, 8]]), gbc.v(0, 64, 8, [[1, 8]]), ALU.mult),
                              [sm.b, gbc.b], [sm.b])
                        yield
                        for u in range(8):
                            P.pe(TR(psv(0, 0, 64, u * 128, [[1, 128]]), uview(kS, u), ident), [kS.b, CF.b], [PB[0], PB[1]])
                        for u in range(8):
                            P.pe(TR(psv(2, 0, 64, u * 128, [[1, 128]]), uview(vS, u), ident), [vS.b, CF.b], [PB[2], PB[3]])
                        yield
                        psK = psv(0, 0, 64, 0, [[128, 8], [1, 128]])
                        psV = psv(2, 0, 64, 0, [[128, 8], [1, 128]])
                        P.dve(TT(rhsW.v(0, 64, 0, [[128, 8], [1, 128]]), psK, sm.v(0, 64, 24, [[1, 8], [0, 128]]), ALU.mult),
                              [PB[0], PB[1], sm.b], [rhsW.b])
                        P.dve(TT(kdec.v(0, 64, 0, [[128, 8], [1, 128]]), psK, sm.v(0, 64, 8, [[1, 8], [0, 128]]), ALU.mult),
                              [PB[0], PB[1], sm.b], [kdec.b])
                        P.dve(TT(rhsV.v(0, 64, 0, [[128, 8], [1, 128]]), psV, beta_bs(128), ALU.mult), [PB[2], PB[3], gbc.b], [rhsV.b])
                        yield
                        for u in range(8):
                            kc_ = uview(kSb, u)
                            P.pe(MM(psv(2, 0, 64, u * 64, [[1, 64]]), kc_, kc_), [kSb.b], [PB[2]])
                        P.pool(TT(Mb.v(0, 64, 0, [[64, 8], [1, 64]]), CF.v(0, 64, C_NSTR, [[0, 8], [1, 64]]), beta_bs(64), ALU.mult),
                               [CF.b, gbc.b], [Mb.b])
                        yield
                        P.dve(TT(A0.full(), psv(2, 0, 64, 0, [[1, 512]]), dec.full(), ALU.mult), [PB[2], dec.b], [A0.b])
                        R0 = Rt[0]
                        P.dve(TT(R0.full(), A0.full(), Mb.full(), ALU.mult), [A0.b, Mb.b], [R0.b])
                        for u in range(8):
                            P.pe(TR(psv(3, 0, 64, u * 64, [[1, 64]]), R0.v(0, 64, u * 64, [[1, 64]]), id64), [R0.b, CF.b], [PB[3]])
                        yield
                        P.act(ACT(PX[0].v(0, 64, 0, [[128, 8], [1, 64]]), psv(3, 0, 64, 0, [[64, 8], [1, 64]]), AF.Copy), [PB[3]], [PX[0].b])
                        P.pool(CP(PX[0].v(0, 64, 64, [[128, 8], [1, 64]]), CF.v(0, 64, C_ID, [[0, 8], [1, 64]])), [CF.b], [PX[0].b])
                        for j in range(6):
                            src, dst = PX[j % 2], PX[(j + 1) % 2]
                            Rs, Rd = Rt[j % 2], Rt[(j + 1) % 2]
                            last = j == 5
                            c0 = 64 if last else 0
                            for u in range(8):
                                P.pe(MM(psv(0, 0, 64, u * 128 + c0, [[1, 128 - c0]]), Rs.v(0, 64, u * 64, [[1, 64]]),
                                        src.v(0, 64, u * 128 + c0, [[1, 128 - c0]])), [Rs.b, src.b], [PB[0], PB[1]])
                            if not last:
                                for u in range(8):
                                    P.pe(MM(psv(2, 0, 64, u * 64, [[1, 64]]), src.v(0, 64, u * 128, [[1, 64]]),
                                            Rs.v(0, 64, u * 64, [[1, 64]])), [Rs.b, src.b], [PB[2]])
                                yield
                                P.act(ACT(dst.v(0, 64, 0, [[128, 8], [1, 64]]), psv(0, 0, 64, 0, [[128, 8], [1, 64]]), AF.Copy),
                                      [PB[0], PB[1]], [dst.b])
                                P.dve(TT(dst.v(0, 64, 64, [[128, 8], [1, 64]]), src.v(0, 64, 64, [[128, 8], [1, 64]]),
                                         psv(0, 0, 64, 64, [[128, 8], [1, 64]]), ALU.add), [src.b, PB[0], PB[1]], [dst.b])
                                P.act(ACT(Rd.full(), psv(2, 0, 64, 0, [[1, 512]]), AF.Copy), [PB[2]], [Rd.b])
                                yield
                            else:
                                yield
                                P.dve(TT(XSt.v(0, 64, 0, [[64, 8], [1, 64]]), src.v(0, 64, 64, [[128, 8], [1, 64]]),
                                         psv(0, 0, 64, 64, [[128, 8], [1, 64]]), ALU.add), [src.b, PB[0], PB[1]], [XSt.b])
                                yield
                        for u in range(8):
                            P.pe(MM(psv(2, 0, 64, u * 128, [[1, 128]]), XSt.v(0, 64, u * 64, [[1, 64]]), rhsV.v(0, 64, u * 128, [[1, 128]])),
                                 [XSt.b, rhsV.b], [PB[2], PB[3]])
                        for u in range(8):
                            P.pe(MM(psv(0, 0, 128, u * 64, [[1, 64]]), rhsW.v(0, 64, u * 128, [[1, 128]]), XSt.v(0, 64, u * 64, [[1, 64]])),
                                 [XSt.b, rhsW.b], [PB[0]])
                        yield
                        P.act(ACT(uS.full(), psv(2, 0, 64, 0, [[1, 1024]]), AF.Copy), [PB[2], PB[3]], [uS.b])
                        P.act(ACT(wTS.full(), psv(0, 0, 128, 0, [[1, 512]]), AF.Copy), [PB[0]], [wTS.b])
                        for u in range(8):
                            P.pe(MM(psv(1, 0, 64, u * 64, [[1, 64]]), uview(kSb, u), uview(qSb, u)), [kSb.b, qSb.b], [PB[1]])
                        yield
                        P.dve(TT(intraT.full(), psv(1, 0, 64, 0, [[1, 512]]), decT.full(), ALU.mult), [PB[1], decT.b], [intraT.b])
                        P.pool(TT(qdT.v(0, 128, 0, [[256, 2], [64, 4], [1, 64]]), qS.v(0, 128, co, [[64, 2], [512, 4], [1, 64]]),
                                  eGb.v(0, 128, 0, [[256, 2], [64, 4], [1, 64]]), ALU.mult), [qS.b, eGb.b], [qdT.b])
                        yield

                    sidx = [0]

                    def rec(pi):
                        n0 = 2 * pi
                        sg, co = n0 // 8, (n0 % 8) * 64
                        OA = oaS[sg % 2]
                        eGb, kdec, uS, wTS, intraT, qdT = (eGb2[pi % 2], kdec2[pi % 2], uS2[pi % 2], wTS2[pi % 2],
                                                           intraT2[pi % 2], qdT2[pi % 2])
                        if n0 % 8 == 0:
                            hf, tc0 = sg // 2, (sg % 2) * 512
                            P.dma("sp", gS.v(0, 128, 0, [[512, 4], [1, 512]]),
                                  FZ[3][hf * 512:(hf + 1) * 512, tc0:tc0 + 512].rearrange("(h d) t -> d h t", d=128), gS.b, db("FZL3"))
                        for cb in range(2):
                            u0 = cb * 4
                            Scur, Snxt = Sst[sidx[0] % 2], Sst[(sidx[0] + 1) % 2]
                            sidx[0] += 1
                            VN, OS_, O2_, RS_ = vnew[cb], oS[cb], o2[cb], rs[cb]
                            bWS, bO, bS, bN = 4, 5, 4, 5
                            for hh in range(4):
                                P.pe(MM(psv(bWS, 0, 64, hh * 128, [[1, 128]]), wTS.v(0, 128, (u0 + hh) * 64, [[1, 64]]),
                                        Scur.v(0, 128, hh * 128, [[1, 128]])), [wTS.b, Scur.b], [PB[bWS]])
                            yield
                            P.dve(TT(VN.full(), uS.v(0, 64, u0 * 128, [[1, 512]]), psv(bWS, 0, 64, 0, [[1, 512]]), ALU.subtract),
                                  [uS.b, PB[bWS]], [VN.b])
                            yield
                            for hh in range(4):
                                P.pe(MM(psv(bS, 0, 128, hh * 128, [[1, 128]]), kdec.v(0, 64, (u0 + hh) * 128, [[1, 128]]),
                                        VN.v(0, 64, hh * 128, [[1, 128]])), [kdec.b, VN.b], [PB[bS]])
                            for hh in range(4):
                                o = psv(bO, 0, 128, hh * 64, [[1, 64]])
                                P.pe(MM(o, Scur.v(0, 128, hh * 128, [[1, 128]]), qdT.v(0, 128, (u0 + hh) * 64, [[1, 64]]), start=True, stop=False),
                                     [Scur.b, qdT.b], [PB[bO]])
                                P.pe(MM(o, VN.v(0, 64, hh * 128, [[1, 128]]), intraT.v(0, 64, (u0 + hh) * 64, [[1, 64]]), start=False, stop=True),
                                     [VN.b, intraT.b], [PB[bO]])
                            P.dve(TT(Snxt.v(0, 128, 0, [[128, 4], [1, 128]]), Scur.v(0, 128, 0, [[128, 4], [1, 128]]),
                                     eGb.v(0, 128, u0 * 64 + 63, [[64, 4], [0, 128]]), ALU.mult), [Scur.b, eGb.b], [Snxt.b])
                            yield
                            P.dve(TT(Snxt.full(), Snxt.full(), psv(bS, 0, 128, 0, [[1, 512]]), ALU.add), [Snxt.b, PB[bS]], [Snxt.b])
                            P.act(ACT(OS_.full(), psv(bO, 0, 128, 0, [[1, 256]]), AF.Copy), [PB[bO]], [OS_.b])
                            P.act(ACT(O2_.full(), psv(bO, 0, 128, 0, [[1, 256]]), AF.Square), [PB[bO]], [O2_.b])
                            P.pe(MM(psv(bN, 0, 128, 0, [[1, 256]]), ones_f, O2_.full()), [O2_.b, CF.b], [PB[bN]])
                            yield
                            P.act(ACT(RS_.full(), psv(bN, 0, 128, 0, [[1, 256]]), AF.Ln, bias=EPSc, scale=1.0 / 128.0), [PB[bN]], [RS_.b])
                            P.act(ACT(RS_.full(), RS_.full(), AF.Exp, scale=-0.5), [RS_.b], [RS_.b])
                            P.dve(TT(OS_.full(), OS_.full(), RS_.full(), ALU.mult), [OS_.b, RS_.b], [OS_.b])
                            cc_ = co + cb * 64
                            P.dve(STT(OA.v(0, 128, cc_, [[512, 4], [1, 64]]), OS_.v(0, 128, 0, [[64, 4], [1, 64]]), gng,
                                      gS.v(0, 128, cc_, [[512, 4], [1, 64]]), ALU.mult, ALU.mult), [OS_.b, gS.b, PPt.b], [OA.b])
                            yield
                        if n0 % 8 == 6:
                            hf, tc0 = sg // 2, (sg % 2) * 512
                            P.dma("sp", OAm[hf * 512:(hf + 1) * 512, tc0:tc0 + 512].rearrange("(h d) t -> d h t", d=128),
                                  OA.v(0, 128, 0, [[512, 4], [1, 512]]), db("OAm"), OA.b)

                    for _ in pre(0):
                        pass
                    for pi in range(16):
                        gens = [[rec(pi), 1]] + ([[pre(pi + 1), QA]] if pi < 15 else [])
                        while gens:
                            for ent in list(gens):
                                try:
                                    for _ in range(ent[1]):
                                        next(ent[0])
                                except StopIteration:
                                    gens.remove(ent)
                    yield

            def phase_S(l):
                with Frame(P) as fr:
                    Vall = fr.tile("Vall", [128, 16, 512], BF16)
                    qTh = [fr.tile("qTh", [128, T], BF16) for _ in range(2)]
                    kTh = [fr.tile("kTh", [128, T], BF16) for _ in range(2)]
                    nkTh = [fr.tile("nkTh", [128, T], BF16) for _ in range(2)]
                    Et = [fr.tile("E", [128, 512], F32) for _ in range(3)]
                    spb = [fr.tile("spb", [128, 512], BF16) for _ in range(3)]
                    acc = fr.tile("acc", [128, 512], BF16)
                    AT = [fr.tile("AT", [128, 512], BF16) for _ in range(2)]
                    ocT = [fr.tile("ocT", [128, T], BF16) for _ in range(2)]
                    triS = CBt.v(0, 128, CB_TRIS, [[1, 128]])
                    P.dma("sp", Vall.v(0, 128, 0, [[512, 16], [1, 512]]), FSV.rearrange("(b p) c -> p b c", p=128), Vall.b, db("FSVL"))
                    items = [(h, qg, kb) for h in range(4) for qg in range(4) for kb in range(4 * qg + 3, -1, -1)]

                    def stage1(i):
                        h, qg, kb = items[i]
                        Q, K = qTh[h % 2], kTh[h % 2]
                        if qg == 0 and kb == 3:
                            for half in range(2):
                                r0 = half * 512 + h * 128
                                P.dma("sp", Q.v(0, 128, half * TL, [[1, TL]]), FSQ[0][r0:r0 + 128, :], Q.b, db("FSQL0"))
                                P.dma("sp", K.v(0, 128, half * TL, [[1, TL]]), FSQ[1][r0:r0 + 128, :], K.b, db("FSQL1"))
                            P.pool(TS(nkTh[h % 2].full(), K.full(), -1.0, None, ALU.mult), [K.b], [nkTh[h % 2].b])
                        E_, SP_ = Et[i % 3], spb[i % 3]
                        bZ = (0, 1, 6)[i % 3]
                        P.pe(MM(psv(bZ, 0, 128, 0, [[1, 512]]), K.v(0, 128, kb * 128, [[1, 128]]), Q.v(0, 128, qg * 512, [[1, 512]])),
                             [K.b, Q.b], [PB[bZ]])
                        P.act(ACT(E_.full(), psv(bZ, 0, 128, 0, [[1, 512]]), AF.Exp), [PB[bZ]], [E_.b])
                        P.act(ACT(SP_.full(), E_.full(), AF.Ln, bias=1.0), [E_.b], [SP_.b])
                        di = kb - 4 * qg
                        if di >= 0:
                            mk = CBt.v(0, 128, CB_MASK + di * 512, [[1, 512]])
                            P.dve(TT(SP_.full(), SP_.full(), mk, ALU.mult), [SP_.b, CBt.b], [SP_.b])

                    def stage2(i):
                        h, qg, kb = items[i]
                        Q, NK, OC = qTh[h % 2], nkTh[h % 2], ocT[h % 2]
                        kmax = 4 * qg + 3
                        SP_, A_ = spb[i % 3], AT[i % 2]
                        bC = 2 + i % 2
                        bO = 4 + (h * 4 + qg) % 2
                        qv = Q.v(0, 128, qg * 512, [[1, 512]])
                        oC = psv(bC, 0, 128, 0, [[1, 512]])
                        P.pe(MM(oC, triS, SP_.full(), start=True, stop=False), [SP_.b, CBt.b], [PB[bC]])
                        if kb < kmax:
                            P.pe(MM(oC, ones_b, acc.full(), start=False, stop=False), [acc.b, CBt.b], [PB[bC]])
                        P.pe(MM(oC, NK.v(0, 128, kb * 128, [[1, 128]]), qv, start=False, stop=True), [NK.b, Q.b], [PB[bC]])
                        P.act(ACT(A_.full(), oC, AF.Exp, scale=-1.0), [PB[bC]], [A_.b])
                        di = kb - 4 * qg
                        if di >= 0:
                            mk = CBt.v(0, 128, CB_MASK + di * 512, [[1, 512]])
                            P.dve(TT(A_.full(), A_.full(), mk, ALU.mult), [A_.b, CBt.b], [A_.b])
                        P.pe(MM(psv(bO, 0, 128, 0, [[1, 512]]), Vall.v(0, 128, kb * 512 + h * 128, [[1, 128]]), A_.full(),
                                start=(kb == kmax), stop=(kb == 0)), [Vall.b, A_.b], [PB[bO]])
                        if kb > 0:
                            if kb == kmax:
                                P.pool(CP(acc.full(), SP_.full()), [SP_.b], [acc.b])
                            else:
                                P.pool(TT(acc.full(), acc.full(), SP_.full(), ALU.add), [acc.b, SP_.b], [acc.b])
                        else:
                            P.act(ACT(OC.v(0, 128, qg * 512, [[1, 512]]), psv(bO, 0, 128, 0, [[1, 512]]), AF.Copy), [PB[bO]], [OC.b])
                            if qg == 3:
                                for half in range(2):
                                    r0 = half * 512 + h * 128
                                    P.dma("sp", OCm[r0:r0 + 128, :], OC.v(0, 128, half * TL, [[1, TL]]), db("OCm"), OC.b)

                    n = len(items)
                    SK = 2
                    for i in range(n + SK):
                        if i < n:
                            stage1(i)
                        if i >= SK:
                            stage2(i - SK)

            def phase_O(l, xin, xin_b, xout, xout_b):
                wsrc = (w_oa[l], w_ob[l], w_oc[l])
                with Frame(P) as fo:
                    yT = fo.tile("yT", [128, 16, TL], BF16)
                    cnt = 0
                    with Frame(P) as fr:
                        br = [fr.tile("br", [128, 8, TL], BF16) for _ in range(3)]
                        wbr = [[fr.tile("wbr", [128, 8, 512], BF16) for _ in range(3)] for _ in range(2)]
                        gsl = [[fr.tile("gsl", [128, TL], BF16) for _ in range(3)] for _ in range(2)]
                        tA = fr.tile("tA", [128, 512], F32)
                        tB = fr.tile("tB", [128, 512], F32)
                        P.dma("sp", br[0].v(0, 128, 0, [[TL, 8], [1, TL]]), BRA.rearrange("(h d) t -> d h t", d=128), br[0].b, db("FOAm"))
                        P.dma("sp", br[1].v(0, 128, 0, [[TL, 8], [1, TL]]), UBT.rearrange("h d t -> d h t"), br[1].b, db("UBT"))
                        P.dma("sp", br[2].v(0, 128, 0, [[TL, 8], [1, TL]]), BRC.rearrange("(h d) t -> d h t", d=128), br[2].b, db("FOCm"))
                        for js in range(4):
                            WB = wbr[js % 2]
                            for i in range(3):
                                load_slab(WB[i], wsrc[i], 0, 8, js * 512, 512)
                            for jj in range(4):
                                j = js * 4 + jj
                                GS = gsl[j % 2]
                                for i in range(3):
                                    P.dma("sp", GS[i].full(), GATES[i * 16 + j], GS[i].b, db("GATES"))
                                for sub in range(TL // 512):
                                    bks = [(cnt * 3 + i) % 6 for i in range(3)]
                                    cnt += 1
                                    for i in range(3):
                                        for kc in range(8):
                                            P.pe(MM(psv(bks[i], 0, 128, 0, [[1, 512]]), WB[i].v(0, 128, kc * 512 + jj * 128, [[1, 128]]),
                                                    br[i].v(0, 128, kc * TL + sub * 512, [[1, 512]]), start=(kc == 0), stop=(kc == 7)),
                                                 [WB[i].b, br[i].b], [PB[bks[i]]])
                                    gv = [GS[i].v(0, 128, sub * 512, [[1, 512]]) for i in range(3)]
                                    P.dve(TT(tA.full(), psv(bks[0], 0, 128, 0, [[1, 512]]), gv[0], ALU.mult), [PB[bks[0]], GS[0].b], [tA.b])
                                    P.dve(TT(tB.full(), psv(bks[1], 0, 128, 0, [[1, 512]]), gv[1], ALU.mult), [PB[bks[1]], GS[1].b], [tB.b])
                                    P.pool(TT(tA.full(), tA.full(), tB.full(), ALU.add), [tA.b, tB.b], [tA.b])
                                    P.dve(TT(tB.full(), psv(bks[2], 0, 128, 0, [[1, 512]]), gv[2], ALU.mult), [PB[bks[2]], GS[2].b], [tB.b])
                                    P.dve(TT(yT.v(0, 128, j * TL + sub * 512, [[1, 512]]), tA.full(), tB.full(), ALU.add), [tA.b, tB.b], [yT.b])
                    with Frame(P) as fr:
                        wo = [fr.tile("wo", [128, 16, 512], BF16) for _ in range(2)]
                        xp = [fr.tile("xp", [128, 512], F32) for _ in range(2)]
                        xo = [fr.tile("xo", [128, 512], F32) for _ in range(2)]
                        for cs in range(4):
                            WO = wo[cs % 2]
                            load_slab(WO, w_o[l], 0, 16, cs * 512, 512)
                            for ti in range(TL // 128):
                                k = cs * 8 + ti
                                XP, XO = xp[k % 2], xo[k % 2]
                                r0 = ti * 128
                                P.dma("sp", XP.full(), xin[r0:r0 + 128, cs * 512:(cs + 1) * 512], XP.b, xin_b)
                                b = 6 + k % 2
                                for kc in range(16):
                                    P.pe(MM(psv(b, 0, 128, 0, [[1, 512]]), yT.v(0, 128, kc * TL + ti * 128, [[1, 128]]),
                                            WO.v(0, 128, kc * 512, [[1, 512]]), start=(kc == 0), stop=(kc == 15)), [yT.b, WO.b], [PB[b]])
                                P.dve(TT(XO.full(), psv(b, 0, 128, 0, [[1, 512]]), XP.full(), ALU.add), [PB[b], XP.b], [XO.b])
                                P.dma("sp", xout[r0:r0 + 128, cs * 512:(cs + 1) * 512], XO.full(), xout_b, XO.b)

            def phase_F(l, xa, xa_b, xb, xb_b, xc, xc_b):
                pb = l * NPP
                with Frame(P) as fo:
                    h2T = fo.tile("h2T", [128, 16, TL], BF16)
                    w1 = [fo.tile("w1", [128, 16, 512], BF16) for _ in range(2)]
                    rl = [fo.tile("rl", [128, 512], F32) for _ in range(2)]
                    w2 = [fo.tile("w2", [128, 32, 256], BF16) for _ in range(2)]
                    xp = [fo.tile("xp", [128, 256], F32) for _ in range(2)]
                    xo = [fo.tile("xo", [128, 256], F32) for _ in range(2)]
                    c1 = 0
                    c2 = 0
                    with Frame(P) as fn:
                        norm_T(fn, xa, xa_b, 0, TL // 128, pb + 16, h2T, 0)
                    for half in range(2):
                        src, src_b, dst, dst_b = (xa, xa_b, xb, xb_b) if half == 0 else (xb, xb_b, xc, xc_b)
                        with Frame(P) as fa:
                            aT = fa.tile("aT", [128, 32, TL], BF16)
                            if True:
                                for s_ in range(8):
                                    W1 = w1[(half * 8 + s_) % 2]
                                    load_slab(W1, w_f1[l], 0, 16, half * 4096 + s_ * 512, 512)
                                    for cg in range(4):
                                        fc = s_ * 4 + cg
                                        for sub in range(TL // 512):
                                            b = c1 % 4
                                            RL = rl[c1 % 2]
                                            c1 += 1
                                            for kc in range(16):
                                                P.pe(MM(psv(b, 0, 128, 0, [[1, 512]]), W1.v(0, 128, kc * 512 + cg * 128, [[1, 128]]),
                                                        h2T.v(0, 128, kc * TL + sub * 512, [[1, 512]]), start=(kc == 0), stop=(kc == 15)),
                                                     [W1.b, h2T.b], [PB[b]])
                                            P.act(ACT(RL.full(), psv(b, 0, 128, 0, [[1, 512]]), AF.Relu), [PB[b]], [RL.b])
                                            P.dve(TT(aT.v(0, 128, fc * TL + sub * 512, [[1, 512]]), RL.full(), RL.full(), ALU.mult), [RL.b], [aT.b])
                            if True:
                                for cs in range(8):
                                    W2 = w2[(half * 8 + cs) % 2]
                                    src_w = w_f2[l][half * 4096:(half + 1) * 4096, cs * 256:(cs + 1) * 256].rearrange("(kc p) c -> p kc c", p=128)
                                    P.dma("pool", W2.v(0, 128, 0, [[256, 32], [1, 256]]), src_w, W2.b, db("w"))
                                    for ti in range(TL // 128):
                                        k = c2
                                        c2 += 1
                                        XP, XO = xp[k % 2], xo[k % 2]
                                        r0 = ti * 128
                                        P.dma("sp", XP.full(), src[r0:r0 + 128, cs * 256:(cs + 1) * 256], XP.b, src_b)
                                        b = 4 + k % 4
                                        for fc in range(32):
                                            P.pe(MM(psv(b, 0, 128, 0, [[1, 256]]), aT.v(0, 128, fc * TL + ti * 128, [[1, 128]]),
                                                    W2.v(0, 128, fc * 256, [[1, 256]]), start=(fc == 0), stop=(fc == 31)), [aT.b, W2.b], [PB[b]])
                                        P.dve(TT(XO.full(), psv(b, 0, 128, 0, [[1, 256]]), XP.full(), ALU.add), [PB[b], XP.b], [XO.b])
                                        P.dma("sp", dst[r0:r0 + 128, cs * 256:(cs + 1) * 256], XO.full(), dst_b, XO.b)

            cur, cur_b = x_in, db("x")
            for l in range(n_layers):
                with Frame(P) as fh:
                    hT = fh.tile("hT", [128, 16, TL], BF16)
                    with Frame(P) as fn:
                        norm_T(fn, cur, cur_b, 0, TL // 128, l * NPP + 0, hT, 0)
                    phase_P(l, hT)
                    if stop_after == "P":
                        break
                    with Frame(P) as fd:
                        xi = phase_X1(0)
                        cnt_ = 0
                        for _ in gates_stream(l, hT, fd):
                            if cnt_ % XQ == 0 and xi is not None:
                                try:
                                    next(xi)
                                except StopIteration:
                                    xi = None
                            cnt_ += 1
                        if xi is not None:
                            for _ in xi:
                                pass
                        for _ in phase_X1(1):
                            pass
                    if stop_after == "X1":
                        break
                for _ in phase_G0(l):
                    pass
                for _ in phase_G(l):
                    pass
                if stop_after == "G":
                    break
                phase_S(l)
                with Frame(P):
                    phase_X2()
                if stop_after == "S":
                    break
                last = (l == n_layers - 1)
                phase_O(l, cur, cur_b, XS[0], db("XS0"))
                if stop_after == "O":
                    break
                outT, outB = (y_out, db("y")) if last else (XS[2], db("XS2"))
                phase_F(l, XS[0], db("XS0"), XS[1], db("XS1"), outT, outB)
                cur, cur_b = XS[2], db("XS2")
        P.emit()
    return nc


def _host_inputs(inputs):
    f = lambda k: np.ascontiguousarray(np.asarray(inputs[k], dtype=np.float32))
    cf, cb = _const_packs()
    pp = np.zeros((128, L * NPP), np.float32)
    nm, nl = f("norm_mix_g"), f("norm_mlp_g")
    cw = f("conv_w")
    cwp = np.zeros((L, 2, 128, 48), np.float32)
    for l in range(L):
        o = l * NPP
        pp[:, o:o + 16] = nm[l].reshape(16, 128).T
        pp[:, o + 16:o + 32] = nl[l].reshape(16, 128).T
        pp[:, o + 128] = f("gdn_norm_g")[l]
        pp[:, o + 129] = f("sba_q_g")[l]
        pp[:, o + 130] = f("sba_k_g")[l]
        c5 = cw[l].reshape(4, 3, 2, 4, 128)
        cwp[l] = c5.transpose(2, 4, 1, 3, 0).reshape(2, 128, 48)
    ab = np.concatenate([f("a_log"), f("dt_bias")], axis=1).reshape(1, L * 16)
    wsT = np.ascontiguousarray(f("w_spatial").transpose(0, 3, 1, 2).reshape(L, 128, 1024))
    shared = {
        "w_in": f("w_in"), "w_out_a": f("w_out_a"), "w_out_b": f("w_out_b"), "w_out_c": f("w_out_c"),
        "w_out": f("w_out"), "w_ff1": f("w_ff1"), "w_ff2": f("w_ff2"),
        "cf": cf, "cb": cb, "pp": pp, "cw": np.ascontiguousarray(cwp.reshape(L * 2 * 128, 48)), "ab": ab,
        "lng": f("gmlp_ln_g"), "wsT": wsT, "bsp": f("b_spatial").reshape(L, 1024),
    }
    return shared


def kernel(**inputs):
    x = np.ascontiguousarray(np.asarray(inputs["x"], dtype=np.float32))
    shared = _host_inputs(inputs)
    nc = build()
    in_maps = [dict(shared, x=np.ascontiguousarray(x[c // 2, (c % 2) * TL:(c % 2 + 1) * TL])) for c in range(NCORES)]
    res = run_bass_kernel_spmd(nc, in_maps, core_ids=list(range(NCORES)))
    out = np.empty((4, T, D), np.float32)
    for c in range(NCORES):
        out[c // 2, (c % 2) * TL:(c % 2 + 1) * TL] = res.results[c]["y"]
    return out
```
